# Optimizing a Trainium2 kernel written in Bass

```python
import jax
import jax.numpy as jnp
from jax import lax
import numpy as np

D_MODEL = 1024
BATCH = 2
SEQ = 16384
DEPTH = 1
DEC_BATCH = 8
DEC_SEQ = 32
PAST_LEN = 1024

CHUNK = 64
N_HEADS_A = 8
HEAD_DIM_A = 64
WIDTH_A = N_HEADS_A * HEAD_DIM_A
N_GROUPS_B = 4
GROUP_DIM_B = 128
WIDTH_B = N_GROUPS_B * GROUP_DIM_B
GMLP_CHUNK = 128
D_FF = 2816
Q_BLOCK = 128
N_MOD = 9
EPS = 1e-6
ATTN_SCALE = HEAD_DIM_A ** -0.5
IN_COLS = 3 * WIDTH_A + N_HEADS_A + 2 * WIDTH_B + 2 * D_MODEL
IN_SPLITS = (WIDTH_A, 2 * WIDTH_A, 3 * WIDTH_A, 3 * WIDTH_A + N_HEADS_A,
             3 * WIDTH_A + N_HEADS_A + 2 * WIDTH_B,
             3 * WIDTH_A + N_HEADS_A + 2 * WIDTH_B + D_MODEL)

kernel_name = 'fox_gmlp_macaron_adaln_stream_step'


def rms_norm(x, g):
    xf = x.astype(jnp.float32)
    y = xf * lax.rsqrt(jnp.mean(xf * xf, axis=-1, keepdims=True) + EPS)
    return (y * g.astype(jnp.float32)).astype(x.dtype)


def modulate(x, shift, scale):
    return x * (1 + scale) + shift


def ada_modulation(c, w_ada, b_ada):
    m = jax.nn.silu(c) @ w_ada + b_ada
    return jnp.split(m[:, None, :], N_MOD, axis=-1)


def ffn_sublayer(x, shift, scale, gate, g, w_gate, w_up, w_down):
    h = modulate(rms_norm(x, g), shift, scale)
    return x + 0.5 * gate * ((jax.nn.silu(h @ w_gate) * (h @ w_up)) @ w_down)


def mixer_projections(n, w_in, b_forget, g_q, g_k, g_gmlp_v):
    B, T, _ = n.shape
    z = n @ w_in
    q, k, v, f, zb, ga, gb = jnp.split(z, IN_SPLITS, axis=-1)
    q = rms_norm(q.reshape(B, T, N_HEADS_A, HEAD_DIM_A), g_q)
    k = rms_norm(k.reshape(B, T, N_HEADS_A, HEAD_DIM_A), g_k)
    v = v.reshape(B, T, N_HEADS_A, HEAD_DIM_A)
    logf = jax.nn.log_sigmoid((f + b_forget).astype(jnp.float32))
    u, vb = jnp.split(jax.nn.gelu(zb), 2, axis=-1)
    vb = rms_norm(vb, g_gmlp_v)
    return q, k, v, logf, u, vb, ga, gb


def fox_prompt(q, k, v, logf):
    B, S, H, Dh = q.shape
    nb = S // Q_BLOCK
    cT = jnp.cumsum(logf, axis=1).transpose(0, 2, 1)
    qb = q.reshape(B, nb, Q_BLOCK, H, Dh).transpose(1, 0, 2, 3, 4)
    cqb = cT.reshape(B, H, nb, Q_BLOCK).transpose(2, 0, 1, 3)
    key_pos = jnp.arange(S)

    def block(args):
        i, qi, ci = args
        s = jnp.einsum('bqhd,bkhd->bhqk', qi, k, preferred_element_type=jnp.float32) * ATTN_SCALE
        s = s + ci[..., :, None] - cT[..., None, :]
        q_pos = i * Q_BLOCK + jnp.arange(Q_BLOCK)
        s = jnp.where(key_pos[None, :] <= q_pos[:, None], s, -jnp.inf)
        p = jax.nn.softmax(s, axis=-1)
        return jnp.einsum('bhqk,bkhd->bqhd', p.astype(v.dtype), v)

    out = lax.map(block, (jnp.arange(nb), qb, cqb))
    return out.transpose(1, 0, 2, 3, 4).reshape(B, S, H * Dh)


def fox_sample(q, k_new, v_new, logf_new, k_cache, v_cache, logf_cache):
    B, T, H, Dh = q.shape
    P = k_cache.shape[1]
    k_all = jnp.concatenate([k_cache.astype(k_new.dtype), k_new], axis=1)
    v_all = jnp.concatenate([v_cache.astype(v_new.dtype), v_new], axis=1)
    lf_all = jnp.concatenate([logf_cache.astype(jnp.float32), logf_new], axis=1)
    cT = jnp.cumsum(lf_all, axis=1).transpose(0, 2, 1)
    s = jnp.einsum('bqhd,bkhd->bhqk', q, k_all, preferred_element_type=jnp.float32) * ATTN_SCALE
    s = s + cT[..., P:, None] - cT[..., None, :]
    key_pos = jnp.arange(P + T)
    q_pos = P + jnp.arange(T)
    s = jnp.where(key_pos[None, :] <= q_pos[:, None], s, -jnp.inf)
    p = jax.nn.softmax(s, axis=-1)
    return jnp.einsum('bhqk,bkhd->bqhd', p.astype(v_all.dtype), v_all).reshape(B, T, H * Dh)


def causal_spatial(w_spatial):
    mask = jnp.tril(jnp.ones((GMLP_CHUNK, GMLP_CHUNK), dtype=bool))
    return jnp.where(mask[None], w_spatial, 0)


def gmlp_prompt(u, vb, w_spatial, b_spatial):
    B, S, _ = vb.shape
    nc = S // GMLP_CHUNK
    vr = vb.reshape(B, nc, GMLP_CHUNK, N_GROUPS_B, GROUP_DIM_B)
    mixed = jnp.einsum('gts,bcsgd->bctgd', causal_spatial(w_spatial), vr)
    mixed = mixed + b_spatial.T[None, None, :, :, None]
    return u * mixed.reshape(B, S, WIDTH_B)


def gmlp_sample(u, vb, w_spatial, b_spatial):
    B, T, _ = vb.shape
    ws = causal_spatial(w_spatial)[:, :T, :T]
    vr = vb.reshape(B, T, N_GROUPS_B, GROUP_DIM_B)
    mixed = jnp.einsum('gts,bsgd->btgd', ws, vr) + b_spatial[:, :T].T[None, :, :, None]
    return u * mixed.reshape(B, T, WIDTH_B)


def merge_branches(a, b, ga, gb, w_proj_a, w_proj_b, w_out):
    m = jax.nn.sigmoid(ga) * (a @ w_proj_a) + jax.nn.sigmoid(gb) * (b @ w_proj_b)
    return m @ w_out


def setup_inputs(seed: int = 0) -> dict:
    key = jax.random.key(seed)
    ks = jax.random.split(key, 28)

    def nrm(k, shape, s):
        return jax.random.normal(k, shape, jnp.float32) * s

    def gain(k, shape):
        return 1.0 + nrm(k, shape, 0.05)

    return {
        'x_prompt': nrm(ks[0], (BATCH, SEQ, D_MODEL), 1.0),
        'x_sample': nrm(ks[1], (DEC_BATCH, DEC_SEQ, D_MODEL), 1.0),
        'c_prompt': nrm(ks[2], (BATCH, D_MODEL), 1.0),
        'c_sample': nrm(ks[3], (DEC_BATCH, D_MODEL), 1.0),
        'cache_fox_k': nrm(ks[4], (DEPTH, DEC_BATCH, PAST_LEN, N_HEADS_A, HEAD_DIM_A), 1.0),
        'cache_fox_v': nrm(ks[5], (DEPTH, DEC_BATCH, PAST_LEN, N_HEADS_A, HEAD_DIM_A), 1.0),
        'cache_fox_logf': jax.nn.log_sigmoid(2.0 + nrm(ks[6], (DEPTH, DEC_BATCH, PAST_LEN, N_HEADS_A), 0.5)),
        'w_ada': nrm(ks[7], (DEPTH, D_MODEL, N_MOD * D_MODEL), 0.5 * D_MODEL ** -0.5),
        'b_ada': nrm(ks[8], (DEPTH, N_MOD * D_MODEL), 0.02),
        'g_norm_ffn1': gain(ks[9], (DEPTH, D_MODEL)),
        'w_ffn1_gate': nrm(ks[10], (DEPTH, D_MODEL, D_FF), D_MODEL ** -0.5),
        'w_ffn1_up': nrm(ks[11], (DEPTH, D_MODEL, D_FF), D_MODEL ** -0.5),
        'w_ffn1_down': nrm(ks[12], (DEPTH, D_FF, D_MODEL), D_FF ** -0.5),
        'g_norm_mix': gain(ks[13], (DEPTH, D_MODEL)),
        'w_in': nrm(ks[14], (DEPTH, D_MODEL, IN_COLS), D_MODEL ** -0.5),
        'b_forget': 2.0 + nrm(ks[15], (DEPTH, N_HEADS_A), 0.5),
        'g_q': gain(ks[16], (DEPTH, HEAD_DIM_A)),
        'g_k': gain(ks[17], (DEPTH, HEAD_DIM_A)),
        'g_gmlp_v': gain(ks[18], (DEPTH, WIDTH_B)),
        'w_spatial': nrm(ks[19], (DEPTH, N_GROUPS_B, GMLP_CHUNK, GMLP_CHUNK), GMLP_CHUNK ** -0.5),
        'b_spatial': 1.0 + nrm(ks[20], (DEPTH, N_GROUPS_B, GMLP_CHUNK), 0.1),
        'w_proj_a': nrm(ks[21], (DEPTH, WIDTH_A, D_MODEL), WIDTH_A ** -0.5),
        'w_proj_b': nrm(ks[22], (DEPTH, WIDTH_B, D_MODEL), WIDTH_B ** -0.5),
        'w_out': nrm(ks[23], (DEPTH, D_MODEL, D_MODEL), D_MODEL ** -0.5),
        'g_norm_ffn2': gain(ks[24], (DEPTH, D_MODEL)),
        'w_ffn2_gate': nrm(ks[25], (DEPTH, D_MODEL, D_FF), D_MODEL ** -0.5),
        'w_ffn2_up': nrm(ks[26], (DEPTH, D_MODEL, D_FF), D_MODEL ** -0.5),
        'w_ffn2_down': nrm(ks[27], (DEPTH, D_FF, D_MODEL), D_FF ** -0.5),
    }


def reference(x_prompt, x_sample, c_prompt, c_sample, cache_fox_k, cache_fox_v, cache_fox_logf,
              w_ada, b_ada, g_norm_ffn1, w_ffn1_gate, w_ffn1_up, w_ffn1_down,
              g_norm_mix, w_in, b_forget, g_q, g_k, g_gmlp_v, w_spatial, b_spatial,
              w_proj_a, w_proj_b, w_out, g_norm_ffn2, w_ffn2_gate, w_ffn2_up, w_ffn2_down):
    xp, xs = x_prompt, x_sample
    kp_l, vp_l, fp_l, ks_l, vs_l, fs_l, gs_l = [], [], [], [], [], [], []
    for l in range(DEPTH):
        mp = ada_modulation(c_prompt, w_ada[l], b_ada[l])
        ms = ada_modulation(c_sample, w_ada[l], b_ada[l])
        ffn1 = (g_norm_ffn1[l], w_ffn1_gate[l], w_ffn1_up[l], w_ffn1_down[l])
        ffn2 = (g_norm_ffn2[l], w_ffn2_gate[l], w_ffn2_up[l], w_ffn2_down[l])
        proj = (w_in[l], b_forget[l], g_q[l], g_k[l], g_gmlp_v[l])
        outp = (w_proj_a[l], w_proj_b[l], w_out[l])

        xp = ffn_sublayer(xp, mp[0], mp[1], mp[2], *ffn1)
        xs = ffn_sublayer(xs, ms[0], ms[1], ms[2], *ffn1)

        n = modulate(rms_norm(xp, g_norm_mix[l]), mp[3], mp[4])
        q, k, v, lf, u, vb, ga, gb = mixer_projections(n, *proj)
        a = fox_prompt(q, k, v, lf)
        b = gmlp_prompt(u, vb, w_spatial[l], b_spatial[l])
        xp = xp + mp[5] * merge_branches(a, b, ga, gb, *outp)
        kp_l.append(k)
        vp_l.append(v)
        fp_l.append(lf)

        n = modulate(rms_norm(xs, g_norm_mix[l]), ms[3], ms[4])
        q, k, v, lf, u, vb, ga, gb = mixer_projections(n, *proj)
        a = fox_sample(q, k, v, lf, cache_fox_k[l], cache_fox_v[l], cache_fox_logf[l])
        b = gmlp_sample(u, vb, w_spatial[l], b_spatial[l])
        xs = xs + ms[5] * merge_branches(a, b, ga, gb, *outp)
        ks_l.append(k)
        vs_l.append(v)
        fs_l.append(lf)
        gs_l.append(vb)

        xp = ffn_sublayer(xp, mp[6], mp[7], mp[8], *ffn2)
        xs = ffn_sublayer(xs, ms[6], ms[7], ms[8], *ffn2)

    return (xp, xs, jnp.stack(kp_l), jnp.stack(vp_l), jnp.stack(fp_l),
            jnp.stack(ks_l), jnp.stack(vs_l), jnp.stack(fs_l), jnp.stack(gs_l))
```

```python
import numpy as np
import ml_dtypes
from contextlib import ExitStack
import concourse.bass as bass
import concourse.mybir as mybir
from concourse.bass_utils import run_bass_kernel_spmd

F32 = mybir.dt.float32
BF16 = mybir.dt.bfloat16
AF = mybir.ActivationFunctionType
ALU = mybir.AluOpType
AX = mybir.AxisListType

D = 1024
DFF = 2816
NF = 22
H = 8
DH = 64
SEQ = 16384
LQ = 4096
TS = 32
PAST = 1024
NROW = SEQ + TS
EPS = 1e-6
INC = 4616
NEG = -30000.0
C_Q, C_K, C_V, C_F, C_ZB, C_GA, C_GB = 0, 512, 1024, 1536, 1544, 2568, 3592
NDMASEM = 6


class Op:
    __slots__ = ("eng", "fn", "deps", "sig", "cnt", "sem", "dma", "semi")

    def __init__(self, eng, fn, dma):
        self.eng = eng
        self.fn = fn
        self.dma = dma
        self.deps = []
        self.sig = False
        self.cnt = 0
        self.sem = None
        self.semi = -1


class Sched:
    ENGS = ["pe", "act", "dve", "pool", "sp"]

    def __init__(self):
        self.ops = {e: [] for e in self.ENGS}
        self.lastw = {}
        self.readers = {}
        self.ndma = {e: 0 for e in self.ENGS}
        self.last_on_sem = {}
        self.all_dma = []

    def add(self, eng, fn, reads=(), writes=(), dma=False):
        op = Op(eng, fn, dma)
        dw = []
        dr = []
        for k in reads:
            w = self.lastw.get(k)
            if w is not None:
                dw.append(w)
        for k in writes:
            w = self.lastw.get(k)
            if w is not None:
                dw.append(w)
            for r in self.readers.get(k, {}).values():
                dr.append(r)
        if dma:
            i = self.ndma[eng] % NDMASEM
            self.ndma[eng] += 1
            op.semi = i
            prev = self.last_on_sem.get((eng, i))
            if prev is not None:
                dw.append(prev)
            self.last_on_sem[(eng, i)] = op
            self.all_dma.append(op)
        deps = []
        seen = set()
        for d in dw:
            if id(d) in seen or d is op:
                continue
            seen.add(id(d))
            if (not d.dma) and d.eng == eng and eng == "pe":
                continue
            deps.append(d)
        for d in dr:
            if id(d) in seen or d is op:
                continue
            seen.add(id(d))
            if (not d.dma) and d.eng == eng and eng == "pe":
                continue
            deps.append(d)
        for d in deps:
            d.sig = True
        op.deps = deps
        for k in reads:
            self.readers.setdefault(k, {})[("d", id(op)) if dma else eng] = op
        for k in writes:
            self.lastw[k] = op
            self.readers[k] = {}
        self.ops[eng].append(op)
        return op

    def barrier(self):
        lasts = []
        for e in self.ENGS:
            for o in reversed(self.ops[e]):
                if (not o.dma) and o.fn is not None:
                    lasts.append(o)
                    break
        lasts += list(self.last_on_sem.values())
        for e in self.ENGS:
            op = Op(e, None, False)
            op.deps = [d for d in lasts if not (d.eng == e and not d.dma)]
            for d in op.deps:
                d.sig = True
            self.ops[e].append(op)

    def emit(self, nc, es):
        esem = {e: es.enter_context(nc.semaphore("s_" + e)) for e in self.ENGS}
        dsem = {}
        for e in self.ENGS:
            if self.ndma[e]:
                for i in range(NDMASEM):
                    dsem[(e, i)] = es.enter_context(nc.semaphore("d_%s%d" % (e, i)))
        for e in self.ENGS:
            c = 0
            dc = {}
            for o in self.ops[e]:
                if o.dma:
                    dc[o.semi] = dc.get(o.semi, 0) + 16
                    o.sem = dsem[(e, o.semi)]
                    o.cnt = dc[o.semi]
                    o.sig = True
                elif o.sig and o.fn is not None:
                    c += 1
                    o.sem = esem[e]
                    o.cnt = c
        finals = {}
        for o in self.all_dma:
            finals[id(o.sem)] = (o.sem, max(o.cnt, finals.get(id(o.sem), (None, 0))[1]))
        block = es.enter_context(nc.Block())
        sched = self

        def run(eng_name, e):
            known = {}
            for o in sched.ops[eng_name]:
                waits = {}
                for d in o.deps:
                    key = id(d.sem)
                    if waits.get(key, (None, 0))[1] < d.cnt:
                        waits[key] = (d.sem, d.cnt)
                for key, (sem, val) in waits.items():
                    if known.get(key, 0) >= val:
                        continue
                    known[key] = val
                    e.wait_ge(sem, val)
                if o.fn is None:
                    continue
                ins = o.fn(e)
                if o.sig:
                    ins.then_inc(o.sem, 16 if o.dma else 1)
            if eng_name == "sp":
                for sem, val in finals.values():
                    e.wait_ge(sem, val)

        @block.tensor
        def _(e):
            run("pe", e)

        @block.scalar
        def _(e):
            run("act", e)

        @block.vector
        def _(e):
            run("dve", e)

        @block.gpsimd
        def _(e):
            run("pool", e)

        @block.sync
        def _(e):
            run("sp", e)


def build_program(kstop=99):
    NTL = SEQ // 128
    NKQ = LQ // 128
    NBQ = LQ // 512
    NPT = PAST // 128
    nc = bass.Bass("TRN2", target_bir_lowering=False)
    S = Sched()
    es = ExitStack()

    def din(name, shape, dt=F32):
        return nc.dram_tensor(name, list(shape), dt, kind="ExternalInput").ap()

    def dout(name, shape):
        return nc.dram_tensor(name, list(shape), F32, kind="ExternalOutput").ap()

    def dscr(name, shape, dt):
        return nc.dram_tensor(name, list(shape), dt, kind="Internal").ap()

    xloc = din("xloc", [SEQ, D])
    xsam = din("xsam", [TS, D])
    cT_d = din("cT", [128, 8, 2])
    valid_d = din("valid", [128, 3])
    ck_d = din("ck", [PAST, 512])
    cv_d = din("cv", [PAST, 512])
    cf_d = din("cf", [PAST, 8])
    w_ada = din("w_ada", [D, 9 * D])
    b_ada = din("b_ada", [1, 9 * D])
    gT_d = din("gT", [128, 3, 8])
    wg_d = [din("wg1", [D, DFF]), din("wg2", [D, DFF])]
    wu_d = [din("wu1", [D, DFF]), din("wu2", [D, DFF])]
    wd_d = [din("wd1", [DFF, D]), din("wd2", [DFF, D])]
    w_in = din("w_in", [D, INC])
    bf_d = din("bfb", [128, 8])
    gq_d = din("gqb", [128, 512])
    gk_d = din("gkb", [128, 512])
    ggv_d = din("ggvb", [128, 512])
    wsT_d = din("wsT", [128, 4, 128])
    bsp_d = din("bspb", [128, 4, 128])
    wpa_d = din("wpa", [512, D])
    wpb_d = din("wpb", [512, D])
    wo_d = din("wo", [D, D])
    ident_d = din("ident", [128, 128])
    cmask_d = din("cmask", [128, 128])
    triu_d = din("triu", [128, 128])
    sut_d = din("sut", [128, 128])
    yp = dout("yp", [LQ, D])
    ys = dout("ys", [TS, D])
    kp = dout("kp", [LQ, 512])
    vp = dout("vp", [LQ, 512])
    fp = dout("fp", [LQ, 8])
    ksn = dout("ksn", [TS, 512])
    vsn = dout("vsn", [TS, 512])
    fsn = dout("fsn", [TS, 8])
    gvs = dout("gvs", [TS, 512])
    X1_d = dscr("X1_d", [NROW, D], F32)
    X2_d = dscr("X2_d", [LQ + TS, D], F32)
    KT_d = dscr("KT_d", [512, SEQ], BF16)
    V_d = dscr("V_d", [SEQ, 512], BF16)
    QT_d = dscr("QT_d", [512, LQ], BF16)
    CP_d = dscr("CP_d", [H, 3, SEQ], BF16)
    AT_d = dscr("AT_d", [512, LQ + TS], BF16)
    KTs_d = dscr("KTs_d", [512, PAST + TS], BF16)
    Vs_d = dscr("Vs_d", [PAST + TS, 512], BF16)
    QTs_d = dscr("QTs_d", [512, TS], BF16)
    CPs_d = dscr("CPs_d", [H, 3, (NPT + 1) * 128], BF16)

    def sb(name, shape, dt):
        return es.enter_context(nc.sbuf_tensor("sb_" + name, list(shape), dt))

    def pstile(name, shape, dt):
        return es.enter_context(nc.psum_tensor("ps_" + name, list(shape), dt))

    ARENA_B = 135168
    arena = sb("arena", [128, ARENA_B // 2], BF16)

    def carve(off, free_shape, dt):
        n = int(np.prod(free_shape))
        esz = 2 if dt == BF16 else 4
        assert off % 4 == 0 and off + n * esz <= ARENA_B, (off, free_shape)
        a = arena[:, off // 2: off // 2 + n * esz // 2]
        if dt == F32:
            a = a.bitcast(F32)
        if len(free_shape) == 2:
            a = a.rearrange("p (a b) -> p a b", b=free_shape[1])
        elif len(free_shape) == 3:
            a = a.rearrange("p (a b c) -> p a b c", b=free_shape[1], c=free_shape[2])
        return a

    Xb = sb("Xb", [128, 5, D], F32)
    XS = {}
    xslot = [0]
    nT = sb("nT", [128, 8, 512], BF16)
    HT = sb("HT", [128, NF, 512], BF16)
    xn = sb("xn", [128, D], BF16)
    sg = [sb("sg%d" % i, [128, 512], BF16) for i in range(2)]
    tmp = [sb("tmp%d" % i, [128, 512], F32) for i in range(2)]
    gateB = sb("gateB", [128, 2, D], F32)
    gcol = sb("gcol", [128, 128], F32)
    modT = sb("modT", [128, 9, 8, 2], F32)
    GT = sb("GT", [128, 3, 8, 2], F32)
    gT = sb("gT", [128, 3, 8], F32)
    ident = sb("ident", [128, 128], BF16)
    identf = sb("identf", [128, 128], F32)
    ss = sb("ss", [128, 8], F32)
    rs = sb("rs", [128, 8], F32)
    LF = sb("LF", [128, NTL, 8], F32)
    LFs = sb("LFs", [128, NPT + 1, 8], F32)
    validB = sb("validB", [128, 3], F32)
    epsT = sb("epsT", [128, 2], F32)
    cT = sb("cT", [128, 8, 2], F32)
    scT = sb("scT", [128, 8, 2], BF16)
    mrow = [Xb[0:2, 0, :], Xb[0:2, 1, :]]
    brow = [Xb[0:2, 2, :], Xb[0:2, 3, :]]
    i2 = sb("i2", [2, 2], F32)

    pb = [pstile("pb%d" % i, [128, 512], F32) for i in range(7)]
    pT = pstile("pT", [128, 8, 128], BF16)

    STQ = "pool"

    def dma(eng, out, in_, reads, writes, **kw):
        return S.add(eng, lambda e: e.dma_start(out=out, in_=in_, **kw), reads=reads, writes=writes, dma=True)

    dma("pool", ident[:, :], ident_d[:, :], [], ["ident"])
    dma("sp", identf[:, :], ident_d[:, :], [], ["identf"])
    dma("sp", cT[:], cT_d[:, :, :], [], ["cT"])
    dma("sp", gT[:], gT_d[:, :, :], [], ["gT"])
    dma("sp", validB[:], valid_d[:, :], [], ["validB"])
    dma("sp", i2[:], ident_d[0:2, 0:2], [], ["i2"])
    S.add("dve", lambda e: e.memset(epsT[:, 0:1], EPS), writes=["epsT"])
    S.add("dve", lambda e: e.memset(epsT[:, 1:2], 1.0), writes=["epsT"])
    S.add("act", lambda e: e.activation(out=scT[:], in_=cT[:], func=AF.Silu), reads=["cT"], writes=["scT"])

    def load_weight_rows(dst, src, nk, cols0, ncols, key, chunk=1024):
        src_v = src.rearrange("(k p) n -> p k n", p=128)
        c = 0
        while c < ncols:
            n = min(chunk, ncols - c)
            dma("pool", dst[:, :, c:c + n], src_v[:, 0:nk, cols0 + c:cols0 + c + n], [], [key])
            c += n

    ffn_w = {}

    def ffn_load_weights(idx):
        Wg = carve(0, [8, DFF], BF16)
        Wu = carve(45056, [8, DFF], BF16)
        Wd = carve(90112, [NF, D], BF16)
        load_weight_rows(Wg[:, :, 0:1408], wg_d[idx], 8, 0, 1408, "Wg0", chunk=1408)
        load_weight_rows(Wu[:, :, 0:1408], wu_d[idx], 8, 0, 1408, "Wu0", chunk=1408)
        load_weight_rows(Wg[:, :, 1408:DFF], wg_d[idx], 8, 1408, 1408, "Wg1", chunk=1408)
        load_weight_rows(Wu[:, :, 1408:DFF], wu_d[idx], 8, 1408, 1408, "Wu1", chunk=1408)
        load_weight_rows(Wd, wd_d[idx], NF, 0, D, "Wd")
        ffn_w[idx] = (Wg, Wu, Wd)

    ffn_load_weights(0)
    NWA = 5
    wa = [HT[:, 4 * b:4 * b + 4, :].rearrange("p a b -> p (a b)").rearrange("p (k n) -> p k n", n=256) for b in range(NWA)]
    w_ada_v = w_ada.rearrange("(k p) n -> p k n", p=128)
    for j in range(36):
        w = wa[j % NWA]
        wk = "wa%d" % (j % NWA)
        dma("pool", w[:, :, :], w_ada_v[:, :, j * 256:(j + 1) * 256], [], [wk])
        i = j // 4
        qq = j % 4
        mr = mrow[i % 2]
        mk = ("X", i % 2)
        pm = pb[j % 2]
        pk = "pb%d" % (j % 2)
        for k in range(8):
            S.add("pe", lambda e, k=k, w=w, pm=pm: e.matmul(out=pm[0:2, 0:256], lhsT=scT[:, k, :], rhs=w[:, k, :],
                                                             start=(k == 0), stop=(k == 7)),
                  reads=[wk, "scT"], writes=[pk])
        if qq == 0:
            for r in range(2):
                dma("sp", brow[i % 2][r:r + 1, :], b_ada[0:1, i * D:(i + 1) * D], [], [("X", 2 + i % 2)])
        S.add("dve", lambda e, pm=pm, mr=mr, qq=qq, i=i: e.tensor_tensor(out=mr[:, qq * 256:(qq + 1) * 256], in0=pm[0:2, 0:256],
                                                                       in1=brow[i % 2][:, qq * 256:(qq + 1) * 256], op=ALU.add),
              reads=[pk, ("X", 2 + i % 2)], writes=[mk])
        if qq == 3:
            pq = pb[2 + (i % 2)]
            pqk = "pb%d" % (2 + (i % 2))
            for c in range(8):
                S.add("pe", lambda e, c=c, mr=mr, pq=pq: e.matmul(out=pq[:, c * 2:c * 2 + 2], lhsT=mr[:, c * 128:(c + 1) * 128],
                                                                 rhs=i2[:, :], start=True, stop=True),
                      reads=[mk, "i2"], writes=[pqk])
            S.add("dve", lambda e, pq=pq, i=i: e.tensor_copy(out=modT[:, i, :, :].rearrange("p c k -> p (c k)"), in_=pq[:, 0:16]),
                  reads=[pqk], writes=["modT"])
    for sub in range(3):
        for kind in range(2):
            S.add("dve", lambda e, sub=sub, kind=kind: e.scalar_tensor_tensor(
                out=GT[:, sub, :, kind], in0=modT[:, 3 * sub + 1, :, kind], scalar=1.0, in1=gT[:, sub, :],
                op0=ALU.add, op1=ALU.mult), reads=["modT", "gT"], writes=["GT"])

    def make_gate(sub, factor):
        for kind in range(2):
            for c in range(8):
                pq = pb[c % 2]
                pqk = "pb%d" % (c % 2)
                S.add("dve", lambda e, c=c, kind=kind: e.tensor_copy(
                    out=gcol[:, :], in_=modT[:, 3 * sub + 2, c, kind:kind + 1].to_broadcast([128, 128])),
                    reads=["modT"], writes=["gcol"])
                S.add("pe", lambda e, pq=pq: e.matmul(out=pq[:, 0:128], lhsT=gcol[:, :], rhs=identf[:, :], start=True, stop=True),
                      reads=["gcol", "identf"], writes=[pqk])
                S.add("act", lambda e, pq=pq, c=c, kind=kind: e.mul(out=gateB[:, kind, c * 128:(c + 1) * 128], in_=pq[:, 0:128], mul=factor),
                      reads=[pqk], writes=["gateB"])

    blocks_all = [(i * 512, 128, 4, 0, i >= 3 * NBQ) for i in range(4 * NBQ)] + [(SEQ, TS, 1, 1, True)]
    blocks_own = [b for b in blocks_all if b[4]]

    def src_rows(src_p, src_s, r0, tp, t):
        if r0 >= SEQ:
            return src_s[0:tp, :]
        return src_p[r0 + t * tp: r0 + (t + 1) * tp, :]

    def load_x(src_p, src_s, blk):
        r0, tp, T, kind, own = blk
        for t in range(T):
            XS[t] = (xslot[0] + t) % 5
        xslot[0] += T
        for t in range(T):
            dma("sp", Xb[:tp, XS[t], :], src_rows(src_p, src_s, r0, tp, t), [], [("X", XS[t])])
        return [XS[t] for t in range(T)]

    def norm_block(blk, sub, slots=None, nTd=None, nTk="nT", lnexp=False, tiles=None, gb=None):
        r0, tp, T, kind, own = blk
        if slots is None:
            slots = [XS[t] for t in range(T)]
        if nTd is None:
            nTd = nT
        for t in (range(T) if tiles is None else tiles):
            sl = slots[t]
            S.add("act", lambda e, t=t, sl=sl: e.activation(out=xn[:tp, :], in_=Xb[:tp, sl, :], func=AF.Square, accum_out=ss[:tp, t:t + 1]),
                  reads=[("X", sl)], writes=["xn", "ss"])
            if lnexp:
                S.add("act", lambda e, t=t: e.activation(out=rs[:tp, t:t + 1], in_=ss[:tp, t:t + 1], func=AF.Ln, scale=1.0 / D, bias=epsT[:tp, 0:1]),
                      reads=["ss", "epsT"], writes=["rs"])
                S.add("act", lambda e, t=t: e.activation(out=rs[:tp, t:t + 1], in_=rs[:tp, t:t + 1], func=AF.Exp, scale=-0.5),
                      reads=["rs"], writes=["rs"])
            else:
                S.add("act", lambda e, t=t: e.activation(out=rs[:tp, t:t + 1], in_=ss[:tp, t:t + 1], func=AF.Sqrt, scale=1.0 / D, bias=epsT[:tp, 0:1]),
                      reads=["ss", "epsT"], writes=["rs"])
                S.add("dve", lambda e, t=t: e.reciprocal(out=rs[:tp, t:t + 1], in_=rs[:tp, t:t + 1]), reads=["rs"], writes=["rs"])
            if gb is not None:
                S.add("dve", lambda e, t=t, sl=sl: e.scalar_tensor_tensor(out=xn[:tp, :], in0=Xb[:tp, sl, :], scalar=rs[:tp, t:t + 1],
                                                                         in1=gb[kind][:tp, :], op0=ALU.mult, op1=ALU.mult),
                      reads=[("X", sl), "rs", "Gb"], writes=["xn"])
                for k in range(8):
                    S.add("pe", lambda e, k=k: e.transpose(out=pT[:, k, :tp], in_=xn[:tp, k * 128:(k + 1) * 128], identity=ident[:tp, :tp]),
                          reads=["xn", "ident"], writes=["pT"])
                S.add("dve", lambda e, t=t: e.tensor_tensor(out=nTd[:, :, t * tp:(t + 1) * tp], in0=pT[:, :, :tp],
                                                            in1=modT[:, 3 * sub, :, kind].unsqueeze(2).to_broadcast([128, 8, tp]), op=ALU.add),
                      reads=["pT", "modT"], writes=[nTk])
                continue
            S.add("dve", lambda e, t=t, sl=sl: e.tensor_scalar(out=xn[:tp, :], in0=Xb[:tp, sl, :], scalar1=rs[:tp, t:t + 1], scalar2=None,
                                                         op0=ALU.mult), reads=[("X", sl), "rs"], writes=["xn"])
            for k in range(8):
                S.add("pe", lambda e, k=k: e.transpose(out=pT[:, k, :tp], in_=xn[:tp, k * 128:(k + 1) * 128], identity=ident[:tp, :tp]),
                      reads=["xn", "ident"], writes=["pT"])
            for k in range(8):
                S.add("act", lambda e, k=k, t=t: e.activation(out=nTd[:, k, t * tp:(t + 1) * tp], in_=pT[:, k, :tp], func=AF.Identity,
                                                               scale=GT[:, sub, k, kind:kind + 1], bias=modT[:, 3 * sub, k, kind:kind + 1]),
                      reads=["pT", "GT", "modT"], writes=[nTk])

    def residual_out(blk, t, half, po, pok, slots=None):
        r0, tp, T, kind, own = blk
        t = XS[t] if slots is None else slots[t]
        tm = tmp[half]
        tk = "tmp%d" % half
        S.add("dve", lambda e: e.tensor_tensor(out=tm[:tp, :], in0=po[:tp, :], in1=gateB[:tp, kind, half * 512:(half + 1) * 512], op=ALU.mult),
              reads=[pok, "gateB"], writes=[tk])
        S.add("dve", lambda e: e.tensor_tensor(out=Xb[:tp, t, half * 512:(half + 1) * 512], in0=tm[:tp, :],
                                               in1=Xb[:tp, t, half * 512:(half + 1) * 512], op=ALU.add),
              reads=[tk, ("X", t)], writes=[("X", t)])

    def ffn_phase(idx, sub, blocks, src_p, src_s, dst_fn):
        if idx not in ffn_w:
            ffn_load_weights(idx)
        Wg, Wu, Wd = ffn_w[idx]
        make_gate(sub, 0.5)
        def do_block(blk):
            r0, tp, T, kind, own = blk
            ntok = tp * T
            load_x(src_p, src_s, blk)
            norm_block(blk, sub)
            for f in range(NF):
                pg = pb[f % 2]
                pu = pb[2 + f % 2]
                pgk = "pb%d" % (f % 2)
                puk = "pb%d" % (2 + f % 2)
                for k in range(8):
                    S.add("pe", lambda e, k=k, f=f, pg=pg: e.matmul(out=pg[:, :ntok], lhsT=Wg[:, k, f * 128:(f + 1) * 128], rhs=nT[:, k, :ntok],
                                                                    start=(k == 0), stop=(k == 7)), reads=["Wg%d" % (f // 11), "nT"], writes=[pgk])
                for k in range(8):
                    S.add("pe", lambda e, k=k, f=f, pu=pu: e.matmul(out=pu[:, :ntok], lhsT=Wu[:, k, f * 128:(f + 1) * 128], rhs=nT[:, k, :ntok],
                                                                    start=(k == 0), stop=(k == 7)), reads=["Wu%d" % (f // 11), "nT"], writes=[puk])
                sgt = sg[f % 2]
                sgk = "sg%d" % (f % 2)
                S.add("act", lambda e, pg=pg, sgt=sgt: e.activation(out=sgt[:, :ntok], in_=pg[:, :ntok], func=AF.Silu), reads=[pgk], writes=[sgk])
                S.add("dve", lambda e, f=f, pu=pu, sgt=sgt: e.tensor_tensor(out=HT[:, f, :ntok], in0=pu[:, :ntok], in1=sgt[:, :ntok], op=ALU.mult),
                      reads=[puk, sgk], writes=["HT"])
            for t in range(T):
                for half in range(2):
                    po = pb[4 + half]
                    pok = "pb%d" % (4 + half)
                    for f in range(NF):
                        S.add("pe", lambda e, f=f, t=t, half=half, po=po: e.matmul(out=po[:tp, :], lhsT=HT[:, f, t * tp:(t + 1) * tp],
                                                                               rhs=Wd[:, f, half * 512:(half + 1) * 512],
                                                                               start=(f == 0), stop=(f == NF - 1)), reads=["HT", "Wd"], writes=[pok])
                    residual_out(blk, t, half, po, pok)
                dma(STQ, dst_fn(blk, t), Xb[:tp, XS[t], :], [("X", XS[t])], [])
        for blk in blocks:
            do_block(blk)
        S.barrier()

    def cumsum(LFt, ntile, nvalid_last, base, CPd, ncol_d, tagk):
        n = ntile * 8
        TT = carve(base, [8], F32)
        R = carve(base + 64, [ntile, 8], F32)
        Cs = carve(base + 64 + 4 * n, [ntile, 8], F32)
        r1 = carve(base + 64 + 8 * n, [ntile, 8], F32)
        p3 = [carve(base + 64 + 12 * n + 2 * n * i, [ntile, 8], BF16) for i in range(3)]
        pf = [carve(base + 64 + 18 * n + 4 * n * i, [ntile, 8], F32) for i in range(2)]
        stg = carve(base + 64 + 26 * n, [3, 128], BF16)
        ones = carve(base + 64 + 26 * n + 768, [128], F32)
        triu = carve(base + 64 + 26 * n + 768 + 512, [128], F32)
        sut = carve(base + 64 + 26 * n + 768 + 1024, [128], F32)
        K = tagk
        S.add("dve", lambda e: e.memset(ones, 1.0), writes=[K + "ones"])
        dma("sp", triu, triu_d[:, :], [], [K + "triu"])
        dma("sp", sut, sut_d[:, :], [], [K + "sut"])
        for h in range(8):
            S.add("pe", lambda e, h=h: e.matmul(out=pb[0][:ntile, h:h + 1], lhsT=LFt[:, :, h], rhs=ones[:, 0:1], start=True, stop=True),
                  reads=[K + "ones"], writes=["pb0"])
        S.add("dve", lambda e: e.tensor_copy(out=TT[:ntile, :], in_=pb[0][:ntile, 0:8]), reads=["pb0"], writes=[K + "TT"])
        S.add("dve", lambda e: e.tensor_tensor(out=R[:ntile, :, :], in0=TT[:ntile, :].unsqueeze(1).to_broadcast([ntile, ntile, 8]),
                                               in1=sut[:ntile, :ntile].unsqueeze(2).to_broadcast([ntile, ntile, 8]), op=ALU.mult),
              reads=[K + "TT", K + "sut"], writes=[K + "R"])
        LF2 = LFt.rearrange("p t h -> p (t h)")
        R2 = R.rearrange("p t h -> p (t h)")
        Cs2 = Cs.rearrange("p t h -> p (t h)")
        c0 = 0
        bi = 1
        while c0 < n:
            w = min(512, n - c0)
            pc = pb[bi]
            pck = "pb%d" % bi
            S.add("pe", lambda e, c0=c0, w=w, pc=pc: e.matmul(out=pc[:, :w], lhsT=triu[:, :], rhs=LF2[:, c0:c0 + w], start=True, stop=False),
                  reads=[K + "triu"], writes=[pck])
            S.add("pe", lambda e, c0=c0, w=w, pc=pc: e.matmul(out=pc[:, :w], lhsT=ones[:ntile, :], rhs=R2[:ntile, c0:c0 + w], start=False, stop=True),
                  reads=[K + "R", K + "ones"], writes=[pck])
            S.add("dve", lambda e, c0=c0, w=w, pc=pc: e.tensor_copy(out=Cs2[:, c0:c0 + w], in_=pc[:, :w]), reads=[pck], writes=[K + "Cs"])
            c0 += w
            bi += 1
        S.add("dve", lambda e: e.tensor_copy(out=p3[0], in_=Cs), reads=[K + "Cs"], writes=[K + "p0"])
        S.add("dve", lambda e: e.tensor_copy(out=pf[0], in_=p3[0]), reads=[K + "p0"], writes=[K + "pf0"])
        S.add("dve", lambda e: e.tensor_tensor(out=r1, in0=Cs, in1=pf[0], op=ALU.subtract), reads=[K + "Cs", K + "pf0"], writes=[K + "r1"])
        S.add("dve", lambda e: e.tensor_copy(out=p3[1], in_=r1), reads=[K + "r1"], writes=[K + "p1"])
        S.add("dve", lambda e: e.tensor_copy(out=pf[1], in_=p3[1]), reads=[K + "p1"], writes=[K + "pf1"])
        S.add("dve", lambda e: e.tensor_tensor(out=r1, in0=r1, in1=pf[1], op=ALU.subtract), reads=[K + "r1", K + "pf1"], writes=[K + "r1"])
        S.add("dve", lambda e: e.tensor_copy(out=p3[2], in_=r1), reads=[K + "r1"], writes=[K + "p2"])
        for h in range(8):
            for i in range(3):
                S.add("pe", lambda e, h=h, i=i: e.transpose(out=pT[:ntile, i, :], in_=p3[i][:, :, h], identity=ident[:, :]),
                      reads=[K + "p%d" % i, "ident"], writes=["pT"])
            S.add("act", lambda e: e.copy(out=stg[:ntile, :, :], in_=pT[:ntile, 0:3, :]), reads=["pT"], writes=[K + "stg"])
            for i in range(3):
                dma("sp", CPd[h, i, 0:ntile * 128].rearrange("(t p) -> t p", p=128), stg[:ntile, i, :], [K + "stg"], [K + "CPd"])

    if kstop == 0:
        S.emit(nc, es)
        es.close()
        return nc
    ffn_phase(0, 0, blocks_all, xloc, xsam,
              lambda blk, t: (X1_d[SEQ:SEQ + TS, :] if blk[0] >= SEQ else X1_d[blk[0] + t * 128: blk[0] + (t + 1) * 128, :]))

    if kstop == 1:
        S.emit(nc, es)
        es.close()
        return nc
    Gb = [HT[:, 4 * kd:4 * kd + 4, :].rearrange("p a b -> p (a b)").bitcast(F32) for kd in range(2)]
    for kd in range(2):
        for c in range(8):
            pq = pb[c % 2]
            pqk = "pb%d" % (c % 2)
            S.add("dve", lambda e, c=c, kd=kd: e.tensor_copy(out=gcol[:, :], in_=GT[:, 1, c, kd:kd + 1].to_broadcast([128, 128])),
                  reads=["GT"], writes=["gcol"])
            S.add("pe", lambda e, pq=pq: e.matmul(out=pq[:, 0:128], lhsT=gcol[:, :], rhs=identf[:, :], start=True, stop=True),
                  reads=["gcol", "identf"], writes=[pqk])
            S.add("act", lambda e, pq=pq, c=c, kd=kd: e.copy(out=Gb[kd][:, c * 128:(c + 1) * 128], in_=pq[:, 0:128]), reads=[pqk], writes=["Gb"])
    Win = carve(0, [8, INC], BF16)
    load_weight_rows(Win, w_in, 8, 0, INC, "Win", chunk=1154)
    cbase = 8 * INC * 2
    gkb = carve(cbase, [512], F32)
    gqb = carve(cbase + 2048, [512], F32)
    bfb = carve(cbase + 6144, [8], F32)
    _o = [cbase + 6176]

    def _c(shape, dt):
        n = int(np.prod(shape)) * (2 if dt == BF16 else 4)
        a = carve(_o[0], shape, dt)
        _o[0] += (n + 3) // 4 * 4
        return a

    A2SCR = _o[0]
    bsets = {}
    for nm in ("k0", "k1", "q0", "q1"):
        bsets[nm] = dict(kn=_c([512], F32), sq=_c([512], F32), b16=_c([512], BF16), ssk=_c([8], F32), tag=nm)
    vfs = [_c([512], F32), _c([512], F32)]
    v16s = [_c([512], BF16), _c([512], BF16)]
    zfs = [_c([8], F32), _c([8], F32)]
    KTss = [_c([4, 512], BF16), _c([4, 512], BF16)]
    QTss = [_c([4, 512], BF16), _c([4, 512], BF16)]
    AEND = _o[0]
    ck16 = carve(A2SCR, [NPT, 512], BF16)
    cv16 = carve(A2SCR + NPT * 1024, [NPT, 512], BF16)
    KTs = KTss[0]
    dma("sp", gkb, gk_d[:, :], [], ["gkb"])
    dma("sp", gqb, gq_d[:, :], [], ["gqb"])
    dma("sp", bfb, bf_d[:, :], [], ["bfb"])
    S.add("dve", lambda e: e.tensor_scalar(out=gqb, in0=gqb, scalar1=DH ** -0.5, scalar2=None, op0=ALU.mult), reads=["gqb"], writes=["gqb"])
    S.add("dve", lambda e: e.memset(LFs[:, :, :], 0.0), writes=["sLF"])

    nT2 = carve(AEND, [8, 512], BF16)
    nTbufs = [(nT, "nT"), (nT2, "nT2")]

    pre_slots = {}

    def a2_pre_load(blk, bidx):
        pre_slots[bidx] = load_x(X1_d, X1_d[SEQ:SEQ + TS, :], blk)

    def a2_pre_norm(blk, bidx, tiles):
        tiles = [t for t in tiles if t < blk[2]]
        if tiles:
            norm_block(blk, 1, pre_slots[bidx], nTbufs[bidx % 2][0], nTbufs[bidx % 2][1], lnexp=True, gb=Gb, tiles=tiles)

    def a2_pre(blk, bidx):
        a2_pre_load(blk, bidx)
        a2_pre_norm(blk, bidx, [0, 1, 2, 3])

    pthalf = [0]

    def a2_block(blk, bidx, nxt=None):
        r0, tp, T, kind, own = blk
        sam = r0 >= SEQ
        bp = bidx % 2
        KTs_, QTs_ = KTss[bp], QTss[bp]
        ktk, qtk = "KTs%d" % bp, "QTs%d" % bp
        nT, nTk_ = nTbufs[bp]

        def pair(tl, mid_hook=None):
            ctx = []
            for t in tl:
                tg = (r0 // 128 + t)
                par = tg % 2
                c = dict(t=t, tg=tg, par=par, Bk=bsets["k%d" % par], Bq=bsets["q%d" % par], vf=vfs[par], v16=v16s[par], zf=zfs[par],
                         vfk="vf%d" % par, v16k="v16%d" % par, zfk="zf%d" % par,
                         pk=(pb[0] if par == 0 else pb[4]), pv=(pb[1] if par == 0 else pb[5]), pq=(pb[3] if par == 0 else pb[6]),
                         pkk=("pb0" if par == 0 else "pb4"), pvk=("pb1" if par == 0 else "pb5"), pqk=("pb3" if par == 0 else "pb6"),
                         fc=par * 8, orow=(r0 - (SEQ - LQ)) + t * 128)
                ctx.append(c)

            def proj(c, ps, psk, c0, ncol, o0=0):
                t = c["t"]
                for k in range(8):
                    S.add("pe", lambda e, k=k: e.matmul(out=ps[:tp, o0:o0 + ncol], lhsT=nT[:, k, t * tp:(t + 1) * tp], rhs=Win[:, k, c0:c0 + ncol],
                                                        start=(k == 0), stop=(k == 7)), reads=[nTk_, "Win"], writes=[psk])
            for c in ctx:
                proj(c, c["pk"], c["pkk"], C_K, 512)
                proj(c, c["pv"], c["pvk"], C_V, 512)
                proj(c, pb[2], "pb2", C_F, 8, c["fc"])
                if own:
                    proj(c, c["pq"], c["pqk"], C_Q, 512)
            if mid_hook is not None:
                mid_hook()
            chains = []
            for c in ctx:
                chains.append((c, c["Bk"], c["pk"], c["pkk"], gkb, "gkb"))
                if own:
                    chains.append((c, c["Bq"], c["pq"], c["pqk"], gqb, "gqb"))
            for (c, B, ps, psk, gain, gk_) in chains:
                S.add("act", lambda e, B=B, ps=ps: e.copy(out=B["kn"][:tp, :], in_=ps[:tp, :]), reads=[psk], writes=[B["tag"] + "kn"])
            for c in ctx:
                S.add("act", lambda e, c=c: e.copy(out=c["vf"][:tp, :], in_=c["pv"][:tp, :]), reads=[c["pvk"]], writes=[c["vfk"]])
            for (c, B, ps, psk, gain, gk_) in chains:
                S.add("act", lambda e, B=B: e.activation(out=B["sq"][:tp, :], in_=B["kn"][:tp, :], func=AF.Square),
                      reads=[B["tag"] + "kn"], writes=[B["tag"] + "sq"])
            for c in ctx:
                S.add("dve", lambda e, c=c: e.tensor_tensor(out=c["zf"][:tp, :], in0=pb[2][:tp, c["fc"]:c["fc"] + 8], in1=bfb[:tp, :], op=ALU.add),
                      reads=["pb2", "bfb"], writes=[c["zfk"]])
            for (c, B, ps, psk, gain, gk_) in chains:
                S.add("dve", lambda e, B=B: e.tensor_reduce(out=B["ssk"][:tp, :], in_=B["sq"][:tp, :].rearrange("p (h d) -> p h d", d=DH), axis=AX.X, op=ALU.add),
                      reads=[B["tag"] + "sq"], writes=[B["tag"] + "ssk"])
            for c in ctx:
                S.add("act", lambda e, c=c: e.activation(out=c["zf"][:tp, :], in_=c["zf"][:tp, :], func=AF.Exp, scale=-1.0), reads=[c["zfk"]], writes=[c["zfk"]])
                if own:
                    dma(STQ, (vsn[:, :] if sam else vp[c["orow"]:c["orow"] + 128, :]), c["vf"][:tp, :], [c["vfk"]], [])
                S.add("dve", lambda e, c=c: e.tensor_copy(out=c["v16"][:tp, :], in_=c["vf"][:tp, :]), reads=[c["vfk"]], writes=[c["v16k"]])
                dma(STQ, (Vs_d[PAST:PAST + TS, :] if sam else V_d[r0 + c["t"] * 128:r0 + (c["t"] + 1) * 128, :]), c["v16"][:tp, :], [c["v16k"]], ["V_d"])
            for c in ctx:
                S.add("act", lambda e, c=c: e.activation(out=c["zf"][:tp, :], in_=c["zf"][:tp, :], func=AF.Ln, bias=epsT[:tp, 1:2]),
                      reads=[c["zfk"], "epsT"], writes=[c["zfk"]])
            for (c, B, ps, psk, gain, gk_) in chains:
                S.add("act", lambda e, B=B: e.activation(out=B["ssk"][:tp, :], in_=B["ssk"][:tp, :], func=AF.Ln, scale=1.0 / DH, bias=epsT[:tp, 0:1]),
                      reads=[B["tag"] + "ssk", "epsT"], writes=[B["tag"] + "ssk"])
            for (c, B, ps, psk, gain, gk_) in chains:
                S.add("act", lambda e, B=B: e.activation(out=B["ssk"][:tp, :], in_=B["ssk"][:tp, :], func=AF.Exp, scale=-0.5),
                      reads=[B["tag"] + "ssk"], writes=[B["tag"] + "ssk"])
            for c in ctx:
                lfdst = LFs[:tp, NPT, :] if sam else LF[:, c["tg"], :]
                lfk = "sLF" if sam else ("LF", c["tg"])
                S.add("dve", lambda e, c=c, lfdst=lfdst: e.tensor_scalar(out=lfdst, in0=c["zf"][:tp, :], scalar1=-1.0, scalar2=None, op0=ALU.mult),
                      reads=[c["zfk"]], writes=[lfk])
                if own:
                    dma(STQ, (fsn[:, :] if sam else fp[c["orow"]:c["orow"] + 128, :]), lfdst, [lfk], [])
            for (c, B, ps, psk, gain, gk_) in chains:
                S.add("dve", lambda e, B=B: e.tensor_tensor(out=B["kn"][:tp, :].rearrange("p (h d) -> p h d", d=DH),
                                                          in0=B["kn"][:tp, :].rearrange("p (h d) -> p h d", d=DH),
                                                          in1=B["ssk"][:tp, :].unsqueeze(2).to_broadcast([tp, 8, DH]), op=ALU.mult),
                      reads=[B["tag"] + "kn", B["tag"] + "ssk"], writes=[B["tag"] + "kn"])
            for (c, B, ps, psk, gain, gk_) in chains:
                S.add("dve", lambda e, B=B, gain=gain: e.tensor_tensor(out=B["kn"][:tp, :], in0=B["kn"][:tp, :], in1=gain[:tp, :], op=ALU.mult),
                      reads=[B["tag"] + "kn", gk_], writes=[B["tag"] + "kn"])
            for (c, B, ps, psk, gain, gk_) in chains:
                if own and gk_ == "gkb":
                    dma(STQ, (ksn[:, :] if sam else kp[c["orow"]:c["orow"] + 128, :]), B["kn"][:tp, :], [B["tag"] + "kn"], [])
                S.add("act", lambda e, B=B: e.copy(out=B["b16"][:tp, :], in_=B["kn"][:tp, :]), reads=[B["tag"] + "kn"], writes=[B["tag"] + "b16"])
            for (c, B, ps, psk, gain, gk_) in chains:
                hh = pthalf[0] % 2
                pthalf[0] += 1
                ptk_ = "pT"
                dstT, dstk = (KTs_, ktk) if gk_ == "gkb" else (QTs_, qtk)
                col0 = c["t"] * tp
                for m in range(4):
                    S.add("pe", lambda e, m=m, B=B, hh=hh: e.transpose(out=pT[:, 4 * hh + m, :tp], in_=B["b16"][:tp, m * 128:(m + 1) * 128], identity=ident[:tp, :tp]),
                          reads=[B["tag"] + "b16", "ident"], writes=[ptk_])
                S.add("dve", lambda e, hh=hh, dstT=dstT, col0=col0: e.tensor_copy(out=dstT[:, :, col0:col0 + tp], in_=pT[:, 4 * hh:4 * hh + 4, :tp]),
                      reads=[ptk_], writes=[dstk])
        tiles = list(range(T))
        for i in range(0, T, 2):
            hook = None
            if nxt is not None:
                hook = (lambda i=i: a2_pre_norm(nxt[0], nxt[1], [i, i + 1] if T > 1 else [0, 1, 2, 3]))
            pair(tiles[i:i + 2], hook)
        if sam:
            dma(STQ, KTs_d[:, PAST:PAST + TS].rearrange("(m p) t -> p m t", p=128), KTs_[:, :, :TS], [ktk], ["KT_d"])
            dma(STQ, QTs_d[:, :].rearrange("(m p) t -> p m t", p=128), QTs_[:, :, :TS], [qtk], ["QT_d"])
        else:
            dma(STQ, KT_d[:, r0:r0 + 512].rearrange("(m p) t -> p m t", p=128), KTs_[:, :, :], [ktk], ["KT_d"])
            if own:
                q0 = r0 - (SEQ - LQ)
                dma(STQ, QT_d[:, q0:q0 + 512].rearrange("(m p) t -> p m t", p=128), QTs_[:, :, :], [qtk], ["QT_d"])
    a2_pre(blocks_all[0], 0)
    for bidx, blk in enumerate(blocks_all):
        nxt = None
        if bidx + 1 < len(blocks_all):
            a2_pre_load(blocks_all[bidx + 1], bidx + 1)
            nxt = (blocks_all[bidx + 1], bidx + 1)
        a2_block(blk, bidx, nxt)
        if nxt is not None and blk[2] < nxt[0][2]:
            a2_pre_norm(nxt[0], nxt[1], [2, 3])
    S.barrier()
    for tt in range(NPT):
        dma("pool", ck16[:, tt, :], ck_d[tt * 128:(tt + 1) * 128, :], [], ["ck16"])
        dma("pool", cv16[:, tt, :], cv_d[tt * 128:(tt + 1) * 128, :], [], ["cv16"])
        dma("sp", LFs[:, tt, :], cf_d[tt * 128:(tt + 1) * 128, :], [], ["sLF"])
    dma("sp", Vs_d[0:PAST, :].rearrange("(t p) d -> p t d", p=128), cv16[:, :, :], ["cv16"], ["V_d"])
    for tt in range(NPT):
        for m in range(4):
            S.add("pe", lambda e, m=m, tt=tt: e.transpose(out=pT[:, m, :], in_=ck16[:, tt, m * 128:(m + 1) * 128], identity=ident[:, :]),
                  reads=["ck16", "ident"], writes=["pT"])
        S.add("dve", lambda e: e.tensor_copy(out=KTs[:, :, 0:128], in_=pT[:, 0:4, :]), reads=["pT"], writes=["KTs"])
        dma("sp", KTs_d[:, tt * 128:(tt + 1) * 128].rearrange("(m p) t -> p m t", p=128), KTs[:, :, 0:128], ["KTs"], ["KT_d"])
    if kstop == 2:
        S.emit(nc, es)
        es.close()
        return nc
    S.barrier()
    cumsum(LF, NTL, 128, cbase, CP_d, SEQ, "c")
    S.barrier()
    cumsum(LFs, NPT + 1, 32, cbase, CPs_d, 1152, "s")
    S.barrier()

    if kstop == 3:
        S.emit(nc, es)
        es.close()
        return nc
    KTx = [carve(0, [SEQ], BF16), None]
    Vx = [carve(32768, [NTL, 128], BF16), carve(65536, [NTL, 128], BF16)]
    QTx = carve(98304, [LQ], BF16)
    PTt = [carve(106496, [512], BF16), carve(107520, [512], BF16), carve(116480, [512], BF16), carve(117504, [512], BF16)]
    num = carve(108544, [512], F32)
    rden = carve(110592, [512], F32)
    rden2 = carve(112640, [512], F32)
    ATs = carve(114688, [512], BF16)
    cm16 = carve(115712, [128], BF16)
    cmf = carve(115968, [128], F32)
    dma("sp", cmf, cmask_d[:, :], [], ["cmf"])
    S.add("dve", lambda e: e.tensor_copy(out=cm16, in_=cmf), reads=["cmf"], writes=["cm16"])
    S.add("dve", lambda e: e.memset(Vx[0][:, :, 64:128], 1.0), writes=["Vx0"])
    S.add("dve", lambda e: e.memset(Vx[1][:, :, 0:64], 1.0), writes=["Vx1"])

    def attention(h, Lk, KTsrc, Vsrc, QTsrc, CPsrc, nq_tot, q_c0, qblocks, ATdst):
        par = h % 2
        Vt = Vx[par]
        vk = "Vx%d" % par
        vc0 = 0 if par == 0 else 64
        nkb = (Lk + 127) // 128
        nfull = Lk // 128
        b0 = 0
        while b0 < nfull:
            nb = min(16, nfull - b0)
            dma("sp", Vt[:, b0:b0 + nb, vc0:vc0 + 64],
                Vsrc[b0 * 128:(b0 + nb) * 128, h * 64:(h + 1) * 64].rearrange("(b p) d -> p b d", p=128), ["V_d"], [vk])
            b0 += nb
        if Lk % 128:
            rem = Lk % 128
            dma("sp", Vt[:rem, nfull, vc0:vc0 + 64], Vsrc[nfull * 128:Lk, h * 64:(h + 1) * 64], ["V_d"], [vk])
        dma("sp", KTx[0][0:64, :Lk], KTsrc[h * 64:(h + 1) * 64, 0:Lk], ["KT_d"], ["KTx"])
        S.add("dve", lambda e: e.memset(KTx[0][64:70, :Lk], 1.0), writes=["KTx"])
        dma("sp", KTx[0][67:70, :Lk], CPsrc[h, :, 0:Lk], ["cCPd", "sCPd"], ["KTx"])
        dma("sp", QTx[0:64, :nq_tot], QTsrc[h * 64:(h + 1) * 64, 0:nq_tot], ["QT_d"], ["QTx"])
        S.add("dve", lambda e: e.memset(QTx[64:70, :nq_tot], -1.0), writes=["QTx"])
        dma("sp", QTx[64:67, :nq_tot], CPsrc[h, :, q_c0:q_c0 + nq_tot], ["cCPd", "sCPd"], ["QTx"])
        flat = []
        for qb_i, (q0, nq, steps) in enumerate(qblocks):
            for si, st in enumerate(steps):
                flat.append((qb_i, q0, nq, si, len(steps), st))
        nflat = len(flat)
        nlo = 0 if par == 0 else 64
        dlo = 64 - nlo

        def emit_qk(gi):
            qb_i, q0, nq, si, ns, (k0, nk, qoff, bias, masked) = flat[gi]
            ps = pb[gi % 4]
            psk = "pb%d" % (gi % 4)
            pt = PTt[gi % 4]
            ptk = "PT%d" % (gi % 4)
            w = nq - qoff
            S.add("pe", lambda e: e.matmul(out=ps[:nk, qoff:qoff + w], lhsT=KTx[0][0:70, k0:k0 + nk],
                                           rhs=QTx[0:70, q0 + qoff:q0 + qoff + w], start=True, stop=(not masked)),
                  reads=["KTx", "QTx"], writes=[psk])
            if masked:
                S.add("pe", lambda e: e.matmul(out=ps[:nk, qoff:qoff + nk], lhsT=ident[:nk, :nk], rhs=cm16[:nk, :nk], start=False, stop=True),
                      reads=["ident", "cm16"], writes=[psk])
            if bias is None:
                S.add("act", lambda e: e.activation(out=pt[:nk, qoff:qoff + w], in_=ps[:nk, qoff:qoff + w], func=AF.Exp),
                      reads=[psk], writes=[ptk])
            else:
                S.add("act", lambda e: e.activation(out=pt[:nk, qoff:qoff + w], in_=ps[:nk, qoff:qoff + w], func=AF.Exp,
                                                    bias=validB[:nk, bias:bias + 1]), reads=[psk, "validB"], writes=[ptk])

        def emit_pv(gi):
            qb_i, q0, nq, si, ns, (k0, nk, qoff, bias, masked) = flat[gi]
            pt = PTt[gi % 4]
            ptk = "PT%d" % (gi % 4)
            po = pb[4 + (qb_i % 2)]
            pok = "pb%d" % (4 + (qb_i % 2))
            w = nq - qoff
            kb = k0 // 128
            S.add("pe", lambda e: e.matmul(out=po[:, qoff:qoff + w], lhsT=Vt[:nk, kb, :], rhs=pt[:nk, qoff:qoff + w],
                                           start=(si == 0), stop=(si == ns - 1)), reads=[vk, ptk], writes=[pok])
            if si == ns - 1:
                S.add("dve", lambda e: e.tensor_copy(out=num[nlo:nlo + 64, :nq], in_=po[nlo:nlo + 64, :nq]), reads=[pok], writes=["num"])
                S.add("dve", lambda e: e.reciprocal(out=rden[dlo:dlo + 64, :nq], in_=po[dlo:dlo + 64, :nq]), reads=[pok], writes=["rden"])
                dma("sp", rden2[nlo:nlo + 64, :nq], rden[dlo:dlo + 64, :nq], ["rden"], ["rden2"])
                S.add("dve", lambda e: e.tensor_tensor(out=ATs[nlo:nlo + 64, :nq], in0=num[nlo:nlo + 64, :nq], in1=rden2[nlo:nlo + 64, :nq],
                                                       op=ALU.mult), reads=["num", "rden2"], writes=["ATs"])
                dma("sp", ATdst[h * 64:(h + 1) * 64, q0:q0 + nq], ATs[nlo:nlo + 64, :nq], ["ATs"], ["AT_d"])

        DEPTH = 3
        for gi in range(nflat + DEPTH):
            if gi < nflat:
                emit_qk(gi)
            if gi >= DEPTH:
                emit_pv(gi - DEPTH)

    pq_blocks = []
    for qi in range(NBQ):
        steps = []
        for j in range(3 * NKQ + 4 * qi):
            steps.append((j * 128, 128, 0, (j // NKQ if j < 3 * NKQ else None), False))
        for a in range(4):
            steps.append(((3 * NKQ + 4 * qi + a) * 128, 128, a * 128, None, True))
        pq_blocks.append((qi * 512, 512, steps))
    for h in range(H):
        attention(h, SEQ, KT_d, V_d, QT_d, CP_d, LQ, SEQ - LQ, pq_blocks, AT_d[:, 0:LQ])
    sq_blocks = [(0, TS, [(j * 128, 128, 0, None, False) for j in range(NPT)] + [(PAST, TS, 0, None, True)])]
    ATs_view = AT_d[:, LQ:LQ + TS]
    for h in range(H):
        attention(h, PAST + TS, KTs_d, Vs_d, QTs_d, CPs_d, TS, PAST, sq_blocks, ATs_view)
    S.barrier()

    if kstop == 4:
        S.emit(nc, es)
        es.close()
        return nc
    Win2 = carve(0, [8, 3072], BF16)
    Wpa = carve(49152, [4, D], BF16)
    Wpb = carve(57344, [4, D], BF16)
    Wo = carve(65536, [8, D], BF16)
    c1 = 81920
    WsT = carve(c1, [4, 128], BF16)
    wsf = carve(c1 + 1024, [4, 128], F32)
    bspB = carve(c1 + 3072, [4, 128], F32)
    triu2 = carve(c1 + 5120, [128], F32)
    ggvb2 = carve(c1 + 5632, [512], F32)
    ATb = carve(c1 + 7680, [4, 512], BF16)
    uT = carve(c1 + 11776, [4, 512], BF16)
    BT = carve(c1 + 15872, [4, 512], BF16)
    MT = carve(c1 + 19968, [8, 512], BF16)
    gz = carve(c1 + 28160, [512], F32)
    g3 = carve(c1 + 30208, [512], F32)
    gs = carve(c1 + 32256, [512], F32)
    gvf = carve(c1 + 34304, [512], F32)
    vb16 = carve(c1 + 36352, [512], BF16)
    t1 = carve(c1 + 37376, [512], F32)
    t2 = carve(c1 + 39424, [512], F32)
    mixb = carve(c1 + 41472, [4, 128], F32)
    load_weight_rows(Win2, w_in, 8, C_ZB, 3072, "Win2")
    load_weight_rows(Wpa, wpa_d, 4, 0, D, "Wpa")
    load_weight_rows(Wpb, wpb_d, 4, 0, D, "Wpb")
    load_weight_rows(Wo, wo_d, 8, 0, D, "Wo")
    dma("sp", wsf, wsT_d[:, :, :], [], ["wsf"])
    dma("sp", bspB, bsp_d[:, :, :], [], ["bspB"])
    dma("sp", triu2, triu_d[:, :], [], ["triu2"])
    dma("sp", ggvb2, ggv_d[:, :], [], ["ggvb2"])
    S.add("dve", lambda e: e.tensor_tensor(out=WsT, in0=wsf, in1=triu2.unsqueeze(1).to_broadcast([128, 4, 128]), op=ALU.mult),
          reads=["wsf", "triu2"], writes=["WsT"])
    make_gate(1, 1.0)

    gsets = [dict(gz=gz, g3=g3, gs=gs, kz="gz", k3="g3", ks="gs"),
             dict(gz=wsf.rearrange("p a b -> p (a b)"), g3=tmp[0], gs=tmp[1], kz="wsf", k3="tmp0", ks="tmp1")]
    gcnt = [0]

    def gelu_tanh(dst, ps, psk, rows, n, dstk):
        G_ = gsets[gcnt[0] % 2]
        gcnt[0] += 1
        gz_, g3_, gs_, kz, k3, ks = G_["gz"], G_["g3"], G_["gs"], G_["kz"], G_["k3"], G_["ks"]
        S.add("act", lambda e: e.copy(out=gz_[:rows, :n], in_=ps[:rows, :n]), reads=[psk], writes=[kz])
        S.add("dve", lambda e: e.tensor_tensor(out=g3_[:rows, :n], in0=gz_[:rows, :n], in1=gz_[:rows, :n], op=ALU.mult), reads=[kz], writes=[k3])
        S.add("dve", lambda e: e.tensor_scalar(out=g3_[:rows, :n], in0=g3_[:rows, :n], scalar1=0.044715, scalar2=1.0, op0=ALU.mult, op1=ALU.add),
              reads=[k3], writes=[k3])
        S.add("dve", lambda e: e.tensor_tensor(out=g3_[:rows, :n], in0=g3_[:rows, :n], in1=gz_[:rows, :n], op=ALU.mult), reads=[k3, kz], writes=[k3])
        S.add("act", lambda e: e.activation(out=gs_[:rows, :n], in_=g3_[:rows, :n], func=AF.Sigmoid, scale=1.5957691216057308),
              reads=[k3], writes=[ks])
        S.add("dve", lambda e: e.tensor_tensor(out=dst, in0=gz_[:rows, :n], in1=gs_[:rows, :n], op=ALU.mult), reads=[kz, ks], writes=[dstk])

    nT2c = carve(126976, [8, 512], BF16)
    nTbc = [(nT, "nT"), (nT2c, "nT2")]

    def c1_pre(blk, bidx):
        r0, tp, T, kind, own = blk
        for t in range(T):
            dma("sp", Xb[:tp, 4, :], src_rows(X1_d, X1_d[SEQ:SEQ + TS, :], r0, tp, t), [], [("X", 4)])
            norm_block(blk, 1, [4] * T, nTbc[bidx % 2][0], nTbc[bidx % 2][1], tiles=[t], gb=Gb)
        return list(range(T))

    def c1_block(blk, bidx, slots):
        r0, tp, T, kind, own = blk
        ntok = tp * T
        sam = r0 >= SEQ
        q0 = LQ if sam else r0 - (SEQ - LQ)
        nT, nTk_ = nTbc[bidx % 2]
        for t in range(T):
            dma("sp", Xb[:tp, slots[t], :], src_rows(X1_d, X1_d[SEQ:SEQ + TS, :], r0, tp, t), [], [("X", slots[t])])
        dma("sp", ATb[:, :, :ntok], AT_d[:, q0:q0 + ntok].rearrange("(m p) t -> p m t", p=128), ["AT_d"], ["ATb"])
        for m in range(4):
            ps = pb[m % 2]
            psk = "pb%d" % (m % 2)
            for k in range(8):
                S.add("pe", lambda e, k=k, m=m, ps=ps: e.matmul(out=ps[:, :ntok], lhsT=Win2[:, k, m * 128:(m + 1) * 128], rhs=nT[:, k, :ntok],
                                                                start=(k == 0), stop=(k == 7)), reads=["Win2", nTk_], writes=[psk])
            gelu_tanh(uT[:, m, :ntok], ps, psk, 128, ntok, "uT")
        for t in range(T):
            pvb = pb[2] if t % 2 == 0 else pb[6]
            pvbk = "pb2" if t % 2 == 0 else "pb6"
            for k in range(8):
                S.add("pe", lambda e, k=k, t=t, pvb=pvb: e.matmul(out=pvb[:tp, :], lhsT=nT[:, k, t * tp:(t + 1) * tp], rhs=Win2[:, k, 512:1024],
                                                          start=(k == 0), stop=(k == 7)), reads=["Win2", nTk_], writes=[pvbk])
            gelu_tanh(gvf[:tp, :], pvb, pvbk, tp, 512, "gvf")
            S.add("act", lambda e: e.activation(out=t1[:tp, :], in_=gvf[:tp, :], func=AF.Square, accum_out=ss[:tp, 4:5]), reads=["gvf"], writes=["t1", "ss"])
            S.add("act", lambda e: e.activation(out=rs[:tp, 4:5], in_=ss[:tp, 4:5], func=AF.Sqrt, scale=1.0 / 512, bias=epsT[:tp, 0:1]),
                  reads=["ss", "epsT"], writes=["rs"])
            S.add("dve", lambda e: e.reciprocal(out=rs[:tp, 4:5], in_=rs[:tp, 4:5]), reads=["rs"], writes=["rs"])
            S.add("dve", lambda e: e.scalar_tensor_tensor(out=gvf[:tp, :], in0=gvf[:tp, :], scalar=rs[:tp, 4:5], in1=ggvb2[:tp, :],
                                                          op0=ALU.mult, op1=ALU.mult), reads=["gvf", "rs", "ggvb2"], writes=["gvf"])
            if sam:
                dma(STQ, gvs[:, :], gvf[:tp, :], ["gvf"], [])
            S.add("act", lambda e: e.copy(out=vb16[:tp, :], in_=gvf[:tp, :]), reads=["gvf"], writes=["vb16"])
            for g in range(4):
                S.add("pe", lambda e, g=g: e.matmul(out=pb[3][:, g * 128:g * 128 + tp], lhsT=vb16[:tp, g * 128:(g + 1) * 128], rhs=WsT[:tp, g, :tp],
                                                    start=True, stop=True), reads=["vb16", "WsT"], writes=["pb3"])
            S.add("dve", lambda e: e.tensor_tensor(out=mixb[:, :, :tp], in0=pb[3][:, :].rearrange("p (g t) -> p g t", t=128)[:, :, :tp],
                                                   in1=bspB[:, :, :tp], op=ALU.add), reads=["pb3", "bspB"], writes=["mixb"])
            S.add("dve", lambda e, t=t: e.tensor_tensor(out=BT[:, :, t * tp:(t + 1) * tp], in0=mixb[:, :, :tp], in1=uT[:, :, t * tp:(t + 1) * tp], op=ALU.mult),
                  reads=["mixb", "uT"], writes=["BT"])
        for m in range(8):
            for br in range(2):
                pgt = pb[br * 2]
                pmt = pb[br * 2 + 1]
                pgk = "pb%d" % (br * 2)
                pmk = "pb%d" % (br * 2 + 1)
                gc0 = (1024 if br == 0 else 2048) + m * 128
                Wp = Wpa if br == 0 else Wpb
                wpk = "Wpa" if br == 0 else "Wpb"
                act_in = ATb if br == 0 else BT
                aik = "ATb" if br == 0 else "BT"
                for k in range(8):
                    S.add("pe", lambda e, k=k, pgt=pgt, gc0=gc0: e.matmul(out=pgt[:, :ntok], lhsT=Win2[:, k, gc0:gc0 + 128], rhs=nT[:, k, :ntok],
                                                                        start=(k == 0), stop=(k == 7)), reads=["Win2", nTk_], writes=[pgk])
                for k in range(4):
                    S.add("pe", lambda e, k=k, pmt=pmt, Wp=Wp, act_in=act_in, m=m: e.matmul(out=pmt[:, :ntok], lhsT=Wp[:, k, m * 128:(m + 1) * 128],
                                                                                          rhs=act_in[:, k, :ntok], start=(k == 0), stop=(k == 3)),
                          reads=[wpk, aik], writes=[pmk])
                sgt = sg[br]
                sgk = "sg%d" % br
                S.add("act", lambda e, pgt=pgt, sgt=sgt: e.activation(out=sgt[:, :ntok], in_=pgt[:, :ntok], func=AF.Sigmoid), reads=[pgk], writes=[sgk])
                tt_ = t1 if br == 0 else t2
                ttk = "t1" if br == 0 else "t2"
                S.add("dve", lambda e, pmt=pmt, sgt=sgt, tt_=tt_: e.tensor_tensor(out=tt_[:, :ntok], in0=pmt[:, :ntok], in1=sgt[:, :ntok], op=ALU.mult),
                      reads=[pmk, sgk], writes=[ttk])
            S.add("dve", lambda e, m=m: e.tensor_tensor(out=MT[:, m, :ntok], in0=t1[:, :ntok], in1=t2[:, :ntok], op=ALU.add),
                  reads=["t1", "t2"], writes=["MT"])
        for t in range(T):
            for half in range(2):
                po = pb[4 + half]
                pok = "pb%d" % (4 + half)
                for k in range(8):
                    S.add("pe", lambda e, k=k, t=t, half=half, po=po: e.matmul(out=po[:tp, :], lhsT=MT[:, k, t * tp:(t + 1) * tp],
                                                                           rhs=Wo[:, k, half * 512:(half + 1) * 512], start=(k == 0), stop=(k == 7)),
                          reads=["MT", "Wo"], writes=[pok])
                residual_out(blk, t, half, po, pok, slots)
            dma(STQ, (X2_d[LQ:LQ + TS, :] if sam else X2_d[q0 + t * 128:q0 + (t + 1) * 128, :]), Xb[:tp, slots[t], :], [("X", slots[t])], [])
    _sl = c1_pre(blocks_own[0], 0)
    for bidx, blk in enumerate(blocks_own):
        _cur = _sl
        if bidx + 1 < len(blocks_own):
            _sl = c1_pre(blocks_own[bidx + 1], bidx + 1)
        c1_block(blk, bidx, _cur)
    S.barrier()

    if kstop == 5:
        S.emit(nc, es)
        es.close()
        return nc
    own_blocks2 = [((r0 - (SEQ - LQ)) if r0 < SEQ else SEQ, tp, T, kind, own) for (r0, tp, T, kind, own) in blocks_own]
    ffn_phase(1, 2, own_blocks2, X2_d, X2_d[LQ:LQ + TS, :],
              lambda blk, t: (ys[:, :] if blk[0] >= SEQ else yp[blk[0] + t * 128: blk[0] + (t + 1) * 128, :]))

    S.emit(nc, es)
    es.close()
    return nc


_NC_CACHE = {}
_KSTOP = 99


def _bf(x):
    return np.ascontiguousarray(x, dtype=np.float32)


def kernel(x_prompt, x_sample, c_prompt, c_sample, cache_fox_k, cache_fox_v, cache_fox_logf,
           w_ada, b_ada, g_norm_ffn1, w_ffn1_gate, w_ffn1_up, w_ffn1_down,
           g_norm_mix, w_in, b_forget, g_q, g_k, g_gmlp_v, w_spatial, b_spatial,
           w_proj_a, w_proj_b, w_out, g_norm_ffn2, w_ffn2_gate, w_ffn2_up, w_ffn2_down):
    f = lambda a: np.asarray(a, dtype=np.float32)
    x_prompt, x_sample, c_prompt, c_sample = f(x_prompt), f(x_sample), f(c_prompt), f(c_sample)
    cache_fox_k, cache_fox_v, cache_fox_logf = f(cache_fox_k), f(cache_fox_v), f(cache_fox_logf)
    if "nc" not in _NC_CACHE:
        _NC_CACHE["nc"] = build_program(_KSTOP)
    nc = _NC_CACHE["nc"]
    ident = np.eye(128, dtype=np.float32)
    idx = np.arange(128)
    cmask = np.where(idx[:, None] > idx[None, :], NEG, 0.0).astype(np.float32)
    triu = (idx[:, None] <= idx[None, :]).astype(np.float32)
    sut = (idx[:, None] < idx[None, :]).astype(np.float32)
    rep = lambda v, n: np.ascontiguousarray(np.broadcast_to(np.tile(f(v), n)[None, :], (128, v.size * n)))
    gT = np.stack([f(g_norm_ffn1)[0].reshape(8, 128).T, f(g_norm_mix)[0].reshape(8, 128).T, f(g_norm_ffn2)[0].reshape(8, 128).T], axis=1)
    common = {
        "w_ada": _bf(f(w_ada)[0]), "b_ada": _bf(f(b_ada)), "gT": _bf(gT),
        "wg1": _bf(f(w_ffn1_gate)[0]), "wu1": _bf(f(w_ffn1_up)[0]), "wd1": _bf(f(w_ffn1_down)[0]),
        "wg2": _bf(f(w_ffn2_gate)[0]), "wu2": _bf(f(w_ffn2_up)[0]), "wd2": _bf(f(w_ffn2_down)[0]),
        "w_in": _bf(f(w_in)[0]),
        "bfb": rep(f(b_forget)[0], 1), "gqb": rep(f(g_q)[0], 8), "gkb": rep(f(g_k)[0], 8), "ggvb": rep(f(g_gmlp_v)[0], 1),
        "wsT": _bf(np.transpose(f(w_spatial)[0], (2, 0, 1))),
        "bspb": _bf(np.broadcast_to(f(b_spatial)[0][None, :, :], (128, 4, 128))),
        "wpa": _bf(f(w_proj_a)[0]), "wpb": _bf(f(w_proj_b)[0]), "wo": _bf(f(w_out)[0]),
        "ident": ident, "cmask": cmask, "triu": triu, "sut": sut,
    }
    in_maps = []
    for c in range(8):
        b, g = c // 4, c % 4
        order = [(g + 1) % 4, (g + 2) % 4, (g + 3) % 4, g]
        xloc = np.concatenate([x_prompt[b, q * LQ:(q + 1) * LQ] for q in order], axis=0)
        valid = np.zeros((128, 3), np.float32)
        for s in range(3):
            if order[s] > g:
                valid[:, s] = -1e30
        cvec = np.stack([c_prompt[b], c_sample[c]], axis=0)
        cT = np.ascontiguousarray(cvec.reshape(2, 8, 128).transpose(2, 1, 0))
        m = dict(common)
        m.update({
            "xloc": _bf(xloc), "xsam": _bf(x_sample[c]), "cT": _bf(cT), "valid": valid,
            "ck": _bf(cache_fox_k[0, c].reshape(PAST, 512)), "cv": _bf(cache_fox_v[0, c].reshape(PAST, 512)),
            "cf": _bf(cache_fox_logf[0, c]),
        })
        in_maps.append(m)
    res = run_bass_kernel_spmd(nc, in_maps, core_ids=list(range(8)))
    R = res.results
    B = 2
    y_prompt = np.zeros((B, SEQ, D), np.float32)
    new_k = np.zeros((1, B, SEQ, H, DH), np.float32)
    new_v = np.zeros((1, B, SEQ, H, DH), np.float32)
    new_f = np.zeros((1, B, SEQ, H), np.float32)
    y_sample = np.zeros((8, TS, D), np.float32)
    ks = np.zeros((1, 8, TS, H, DH), np.float32)
    vs = np.zeros((1, 8, TS, H, DH), np.float32)
    fs = np.zeros((1, 8, TS, H), np.float32)
    gv = np.zeros((1, 8, TS, 512), np.float32)
    for c in range(8):
        b, g = c // 4, c % 4
        sl = slice(g * LQ, (g + 1) * LQ)
        y_prompt[b, sl] = R[c]["yp"]
        new_k[0, b, sl] = R[c]["kp"].reshape(LQ, H, DH)
        new_v[0, b, sl] = R[c]["vp"].reshape(LQ, H, DH)
        new_f[0, b, sl] = R[c]["fp"]
        y_sample[c] = R[c]["ys"]
        ks[0, c] = R[c]["ksn"].reshape(TS, H, DH)
        vs[0, c] = R[c]["vsn"].reshape(TS, H, DH)
        fs[0, c] = R[c]["fsn"]
        gv[0, c] = R[c]["gvs"]
    return (y_prompt, y_sample, new_k, new_v, new_f, ks, vs, fs, gv)
```

```python
import numpy as np
import ml_dtypes
from contextlib import ExitStack
import concourse.bass as bass
import concourse.mybir as mybir
from concourse.bass_utils import run_bass_kernel_spmd

F32 = mybir.dt.float32
BF16 = mybir.dt.bfloat16
AF = mybir.ActivationFunctionType
ALU = mybir.AluOpType
AX = mybir.AxisListType

D = 1024
DFF = 2816
NF = 22
H = 8
DH = 64
SEQ = 16384
LQ = 4096
TS = 32
PAST = 1024
NROW = SEQ + TS
EPS = 1e-6
INC = 4616
NEG = -30000.0
C_Q, C_K, C_V, C_F, C_ZB, C_GA, C_GB = 0, 512, 1024, 1536, 1544, 2568, 3592
NDMASEM = 6


class Op:
    __slots__ = ("eng", "fn", "deps", "sig", "cnt", "sem", "dma", "semi")

    def __init__(self, eng, fn, dma):
        self.eng = eng
        self.fn = fn
        self.dma = dma
        self.deps = []
        self.sig = False
        self.cnt = 0
        self.sem = None
        self.semi = -1


class Sched:
    ENGS = ["pe", "act", "dve", "pool", "sp"]

    def __init__(self):
        self.ops = {e: [] for e in self.ENGS}
        self.lastw = {}
        self.readers = {}
        self.ndma = {e: 0 for e in self.ENGS}
        self.last_on_sem = {}
        self.all_dma = []

    def add(self, eng, fn, reads=(), writes=(), dma=False):
        op = Op(eng, fn, dma)
        dw = []
        dr = []
        for k in reads:
            w = self.lastw.get(k)
            if w is not None:
                dw.append(w)
        for k in writes:
            w = self.lastw.get(k)
            if w is not None:
                dw.append(w)
            for r in self.readers.get(k, {}).values():
                dr.append(r)
        if dma:
            i = self.ndma[eng] % NDMASEM
            self.ndma[eng] += 1
            op.semi = i
            prev = self.last_on_sem.get((eng, i))
            if prev is not None:
                dw.append(prev)
            self.last_on_sem[(eng, i)] = op
            self.all_dma.append(op)
        deps = []
        seen = set()
        for d in dw:
            if id(d) in seen or d is op:
                continue
            seen.add(id(d))
            if (not d.dma) and d.eng == eng and eng == "pe":
                continue
            deps.append(d)
        for d in dr:
            if id(d) in seen or d is op:
                continue
            seen.add(id(d))
            if (not d.dma) and d.eng == eng and eng == "pe":
                continue
            deps.append(d)
        for d in deps:
            d.sig = True
        op.deps = deps
        for k in reads:
            self.readers.setdefault(k, {})[("d", id(op)) if dma else eng] = op
        for k in writes:
            self.lastw[k] = op
            self.readers[k] = {}
        self.ops[eng].append(op)
        return op

    def barrier(self):
        lasts = []
        for e in self.ENGS:
            for o in reversed(self.ops[e]):
                if (not o.dma) and o.fn is not None:
                    lasts.append(o)
                    break
        lasts += list(self.last_on_sem.values())
        for e in self.ENGS:
            op = Op(e, None, False)
            op.deps = [d for d in lasts if not (d.eng == e and not d.dma)]
            for d in op.deps:
                d.sig = True
            self.ops[e].append(op)

    def emit(self, nc, es):
        esem = {e: es.enter_context(nc.semaphore("s_" + e)) for e in self.ENGS}
        dsem = {}
        for e in self.ENGS:
            if self.ndma[e]:
                for i in range(NDMASEM):
                    dsem[(e, i)] = es.enter_context(nc.semaphore("d_%s%d" % (e, i)))
        for e in self.ENGS:
            c = 0
            dc = {}
            for o in self.ops[e]:
                if o.dma:
                    dc[o.semi] = dc.get(o.semi, 0) + 16
                    o.sem = dsem[(e, o.semi)]
                    o.cnt = dc[o.semi]
                    o.sig = True
                elif o.sig and o.fn is not None:
                    c += 1
                    o.sem = esem[e]
                    o.cnt = c
        finals = {}
        for o in self.all_dma:
            finals[id(o.sem)] = (o.sem, max(o.cnt, finals.get(id(o.sem), (None, 0))[1]))
        block = es.enter_context(nc.Block())
        sched = self

        def run(eng_name, e):
            known = {}
            for o in sched.ops[eng_name]:
                waits = {}
                for d in o.deps:
                    key = id(d.sem)
                    if waits.get(key, (None, 0))[1] < d.cnt:
                        waits[key] = (d.sem, d.cnt)
                for key, (sem, val) in waits.items():
                    if known.get(key, 0) >= val:
                        continue
                    known[key] = val
                    e.wait_ge(sem, val)
                if o.fn is None:
                    continue
                ins = o.fn(e)
                if o.sig:
                    ins.then_inc(o.sem, 16 if o.dma else 1)
            if eng_name == "sp":
                for sem, val in finals.values():
                    e.wait_ge(sem, val)

        @block.tensor
        def _(e):
            run("pe", e)

        @block.scalar
        def _(e):
            run("act", e)

        @block.vector
        def _(e):
            run("dve", e)

        @block.gpsimd
        def _(e):
            run("pool", e)

        @block.sync
        def _(e):
            run("sp", e)


def build_program(kstop=99):
    NTL = SEQ // 128
    NKQ = LQ // 128
    NBQ = LQ // 512
    NPT = PAST // 128
    nc = bass.Bass("TRN2", target_bir_lowering=False)
    S = Sched()
    es = ExitStack()

    def din(name, shape, dt=F32):
        return nc.dram_tensor(name, list(shape), dt, kind="ExternalInput").ap()

    def dout(name, shape):
        return nc.dram_tensor(name, list(shape), F32, kind="ExternalOutput").ap()

    def dscr(name, shape, dt):
        return nc.dram_tensor(name, list(shape), dt, kind="Internal").ap()

    xloc = din("xloc", [SEQ, D])
    xsam = din("xsam", [TS, D])
    cT_d = din("cT", [128, 8, 2])
    valid_d = din("valid", [128, 3])
    ck_d = din("ck", [PAST, 512])
    cv_d = din("cv", [PAST, 512])
    cf_d = din("cf", [PAST, 8])
    w_ada = din("w_ada", [D, 9 * D])
    b_ada = din("b_ada", [1, 9 * D])
    gT_d = din("gT", [128, 3, 8])
    wg_d = [din("wg1", [D, DFF]), din("wg2", [D, DFF])]
    wu_d = [din("wu1", [D, DFF]), din("wu2", [D, DFF])]
    wd_d = [din("wd1", [DFF, D]), din("wd2", [DFF, D])]
    w_in = din("w_in", [D, INC])
    bf_d = din("bfb", [128, 8])
    gq_d = din("gqb", [128, 512])
    gk_d = din("gkb", [128, 512])
    ggv_d = din("ggvb", [128, 512])
    wsT_d = din("wsT", [128, 4, 128])
    bsp_d = din("bspb", [128, 4, 128])
    wpa_d = din("wpa", [512, D])
    wpb_d = din("wpb", [512, D])
    wo_d = din("wo", [D, D])
    ident_d = din("ident", [128, 128])
    cmask_d = din("cmask", [128, 128])
    triu_d = din("triu", [128, 128])
    sut_d = din("sut", [128, 128])
    yp = dout("yp", [LQ, D])
    ys = dout("ys", [TS, D])
    kp = dout("kp", [LQ, 512])
    vp = dout("vp", [LQ, 512])
    fp = dout("fp", [LQ, 8])
    ksn = dout("ksn", [TS, 512])
    vsn = dout("vsn", [TS, 512])
    fsn = dout("fsn", [TS, 8])
    gvs = dout("gvs", [TS, 512])
    X1_d = dscr("X1_d", [NROW, D], F32)
    X2_d = dscr("X2_d", [LQ + TS, D], F32)
    KT_d = dscr("KT_d", [512, SEQ], BF16)
    V_d = dscr("V_d", [SEQ, 512], BF16)
    QT_d = dscr("QT_d", [512, LQ], BF16)
    CP_d = dscr("CP_d", [H, 3, SEQ], BF16)
    AT_d = dscr("AT_d", [512, LQ + TS], BF16)
    KTs_d = dscr("KTs_d", [512, PAST + TS], BF16)
    Vs_d = dscr("Vs_d", [PAST + TS, 512], BF16)
    QTs_d = dscr("QTs_d", [512, TS], BF16)
    CPs_d = dscr("CPs_d", [H, 3, (NPT + 1) * 128], BF16)

    def sb(name, shape, dt):
        return es.enter_context(nc.sbuf_tensor("sb_" + name, list(shape), dt))

    def pstile(name, shape, dt):
        return es.enter_context(nc.psum_tensor("ps_" + name, list(shape), dt))

    ARENA_B = 135168
    arena = sb("arena", [128, ARENA_B // 2], BF16)

    def carve(off, free_shape, dt):
        n = int(np.prod(free_shape))
        esz = 2 if dt == BF16 else 4
        assert off % 4 == 0 and off + n * esz <= ARENA_B, (off, free_shape)
        a = arena[:, off // 2: off // 2 + n * esz // 2]
        if dt == F32:
            a = a.bitcast(F32)
        if len(free_shape) == 2:
            a = a.rearrange("p (a b) -> p a b", b=free_shape[1])
        elif len(free_shape) == 3:
            a = a.rearrange("p (a b c) -> p a b c", b=free_shape[1], c=free_shape[2])
        return a

    Xb = sb("Xb", [128, 5, D], F32)
    XS = {}
    xslot = [0]
    nT = sb("nT", [128, 8, 512], BF16)
    HT = sb("HT", [128, NF, 512], BF16)
    xn = sb("xn", [128, D], BF16)
    sg = [sb("sg%d" % i, [128, 512], BF16) for i in range(2)]
    tmp = [sb("tmp%d" % i, [128, 512], F32) for i in range(2)]
    gateB = sb("gateB", [128, 2, D], F32)
    gcol = sb("gcol", [128, 128], F32)
    modT = sb("modT", [128, 9, 8, 2], F32)
    GT = sb("GT", [128, 3, 8, 2], F32)
    gT = sb("gT", [128, 3, 8], F32)
    ident = sb("ident", [128, 128], BF16)
    identf = sb("identf", [128, 128], F32)
    ss = sb("ss", [128, 8], F32)
    rs = sb("rs", [128, 8], F32)
    LF = sb("LF", [128, NTL, 8], F32)
    LFs = sb("LFs", [128, NPT + 1, 8], F32)
    validB = sb("validB", [128, 3], F32)
    epsT = sb("epsT", [128, 2], F32)
    cT = sb("cT", [128, 8, 2], F32)
    scT = sb("scT", [128, 8, 2], BF16)
    mrow = [Xb[0:2, 0, :], Xb[0:2, 1, :]]
    brow = [Xb[0:2, 2, :], Xb[0:2, 3, :]]
    i2 = sb("i2", [2, 2], F32)

    pb = [pstile("pb%d" % i, [128, 512], F32) for i in range(7)]
    pT = pstile("pT", [128, 8, 128], BF16)

    STQ = "pool"

    def dma(eng, out, in_, reads, writes, **kw):
        return S.add(eng, lambda e: e.dma_start(out=out, in_=in_, **kw), reads=reads, writes=writes, dma=True)

    dma("pool", ident[:, :], ident_d[:, :], [], ["ident"])
    dma("sp", identf[:, :], ident_d[:, :], [], ["identf"])
    dma("sp", cT[:], cT_d[:, :, :], [], ["cT"])
    dma("sp", gT[:], gT_d[:, :, :], [], ["gT"])
    dma("sp", validB[:], valid_d[:, :], [], ["validB"])
    dma("sp", i2[:], ident_d[0:2, 0:2], [], ["i2"])
    S.add("dve", lambda e: e.memset(epsT[:, 0:1], EPS), writes=["epsT"])
    S.add("dve", lambda e: e.memset(epsT[:, 1:2], 1.0), writes=["epsT"])
    S.add("act", lambda e: e.activation(out=scT[:], in_=cT[:], func=AF.Silu), reads=["cT"], writes=["scT"])

    def load_weight_rows(dst, src, nk, cols0, ncols, key, chunk=1024):
        src_v = src.rearrange("(k p) n -> p k n", p=128)
        c = 0
        while c < ncols:
            n = min(chunk, ncols - c)
            dma("pool", dst[:, :, c:c + n], src_v[:, 0:nk, cols0 + c:cols0 + c + n], [], [key])
            c += n

    ffn_w = {}

    def ffn_load_weights(idx):
        Wg = carve(0, [8, DFF], BF16)
        Wu = carve(45056, [8, DFF], BF16)
        Wd = carve(90112, [NF, D], BF16)
        load_weight_rows(Wg, wg_d[idx], 8, 0, DFF, "Wg", chunk=1408)
        load_weight_rows(Wu, wu_d[idx], 8, 0, DFF, "Wu", chunk=1408)
        load_weight_rows(Wd, wd_d[idx], NF, 0, D, "Wd")
        ffn_w[idx] = (Wg, Wu, Wd)

    ffn_load_weights(0)
    NWA = 5
    wa = [HT[:, 4 * b:4 * b + 4, :].rearrange("p a b -> p (a b)").rearrange("p (k n) -> p k n", n=256) for b in range(NWA)]
    w_ada_v = w_ada.rearrange("(k p) n -> p k n", p=128)
    for j in range(36):
        w = wa[j % NWA]
        wk = "wa%d" % (j % NWA)
        dma("pool", w[:, :, :], w_ada_v[:, :, j * 256:(j + 1) * 256], [], [wk])
        i = j // 4
        qq = j % 4
        mr = mrow[i % 2]
        mk = ("X", i % 2)
        pm = pb[j % 2]
        pk = "pb%d" % (j % 2)
        for k in range(8):
            S.add("pe", lambda e, k=k, w=w, pm=pm: e.matmul(out=pm[0:2, 0:256], lhsT=scT[:, k, :], rhs=w[:, k, :],
                                                             start=(k == 0), stop=(k == 7)),
                  reads=[wk, "scT"], writes=[pk])
        if qq == 0:
            for r in range(2):
                dma("sp", brow[i % 2][r:r + 1, :], b_ada[0:1, i * D:(i + 1) * D], [], [("X", 2 + i % 2)])
        S.add("dve", lambda e, pm=pm, mr=mr, qq=qq, i=i: e.tensor_tensor(out=mr[:, qq * 256:(qq + 1) * 256], in0=pm[0:2, 0:256],
                                                                       in1=brow[i % 2][:, qq * 256:(qq + 1) * 256], op=ALU.add),
              reads=[pk, ("X", 2 + i % 2)], writes=[mk])
        if qq == 3:
            pq = pb[2 + (i % 2)]
            pqk = "pb%d" % (2 + (i % 2))
            for c in range(8):
                S.add("pe", lambda e, c=c, mr=mr, pq=pq: e.matmul(out=pq[:, c * 2:c * 2 + 2], lhsT=mr[:, c * 128:(c + 1) * 128],
                                                                 rhs=i2[:, :], start=True, stop=True),
                      reads=[mk, "i2"], writes=[pqk])
            S.add("dve", lambda e, pq=pq, i=i: e.tensor_copy(out=modT[:, i, :, :].rearrange("p c k -> p (c k)"), in_=pq[:, 0:16]),
                  reads=[pqk], writes=["modT"])
    for sub in range(3):
        for kind in range(2):
            S.add("dve", lambda e, sub=sub, kind=kind: e.scalar_tensor_tensor(
                out=GT[:, sub, :, kind], in0=modT[:, 3 * sub + 1, :, kind], scalar=1.0, in1=gT[:, sub, :],
                op0=ALU.add, op1=ALU.mult), reads=["modT", "gT"], writes=["GT"])

    def make_gate(sub, factor):
        for kind in range(2):
            for c in range(8):
                pq = pb[c % 2]
                pqk = "pb%d" % (c % 2)
                S.add("dve", lambda e, c=c, kind=kind: e.tensor_copy(
                    out=gcol[:, :], in_=modT[:, 3 * sub + 2, c, kind:kind + 1].to_broadcast([128, 128])),
                    reads=["modT"], writes=["gcol"])
                S.add("pe", lambda e, pq=pq: e.matmul(out=pq[:, 0:128], lhsT=gcol[:, :], rhs=identf[:, :], start=True, stop=True),
                      reads=["gcol", "identf"], writes=[pqk])
                S.add("act", lambda e, pq=pq, c=c, kind=kind: e.mul(out=gateB[:, kind, c * 128:(c + 1) * 128], in_=pq[:, 0:128], mul=factor),
                      reads=[pqk], writes=["gateB"])

    blocks_all = [(i * 512, 128, 4, 0, i >= 3 * NBQ) for i in range(4 * NBQ)] + [(SEQ, TS, 1, 1, True)]
    blocks_own = [b for b in blocks_all if b[4]]

    def src_rows(src_p, src_s, r0, tp, t):
        if r0 >= SEQ:
            return src_s[0:tp, :]
        return src_p[r0 + t * tp: r0 + (t + 1) * tp, :]

    def load_x(src_p, src_s, blk):
        r0, tp, T, kind, own = blk
        for t in range(T):
            XS[t] = (xslot[0] + t) % 5
        xslot[0] += T
        for t in range(T):
            dma("sp", Xb[:tp, XS[t], :], src_rows(src_p, src_s, r0, tp, t), [], [("X", XS[t])])
        return [XS[t] for t in range(T)]

    def norm_block(blk, sub, slots=None, nTd=None, nTk="nT", lnexp=False, tiles=None, gb=None):
        r0, tp, T, kind, own = blk
        if slots is None:
            slots = [XS[t] for t in range(T)]
        if nTd is None:
            nTd = nT
        for t in (range(T) if tiles is None else tiles):
            sl = slots[t]
            S.add("act", lambda e, t=t, sl=sl: e.activation(out=xn[:tp, :], in_=Xb[:tp, sl, :], func=AF.Square, accum_out=ss[:tp, t:t + 1]),
                  reads=[("X", sl)], writes=["xn", "ss"])
            if lnexp:
                S.add("act", lambda e, t=t: e.activation(out=rs[:tp, t:t + 1], in_=ss[:tp, t:t + 1], func=AF.Ln, scale=1.0 / D, bias=epsT[:tp, 0:1]),
                      reads=["ss", "epsT"], writes=["rs"])
                S.add("act", lambda e, t=t: e.activation(out=rs[:tp, t:t + 1], in_=rs[:tp, t:t + 1], func=AF.Exp, scale=-0.5),
                      reads=["rs"], writes=["rs"])
            else:
                S.add("act", lambda e, t=t: e.activation(out=rs[:tp, t:t + 1], in_=ss[:tp, t:t + 1], func=AF.Sqrt, scale=1.0 / D, bias=epsT[:tp, 0:1]),
                      reads=["ss", "epsT"], writes=["rs"])
                S.add("dve", lambda e, t=t: e.reciprocal(out=rs[:tp, t:t + 1], in_=rs[:tp, t:t + 1]), reads=["rs"], writes=["rs"])
            if gb is not None:
                S.add("dve", lambda e, t=t, sl=sl: e.scalar_tensor_tensor(out=xn[:tp, :], in0=Xb[:tp, sl, :], scalar=rs[:tp, t:t + 1],
                                                                         in1=gb[kind][:tp, :], op0=ALU.mult, op1=ALU.mult),
                      reads=[("X", sl), "rs", "Gb"], writes=["xn"])
                for k in range(8):
                    S.add("pe", lambda e, k=k: e.transpose(out=pT[:, k, :tp], in_=xn[:tp, k * 128:(k + 1) * 128], identity=ident[:tp, :tp]),
                          reads=["xn", "ident"], writes=["pT"])
                S.add("dve", lambda e, t=t: e.tensor_tensor(out=nTd[:, :, t * tp:(t + 1) * tp], in0=pT[:, :, :tp],
                                                            in1=modT[:, 3 * sub, :, kind].unsqueeze(2).to_broadcast([128, 8, tp]), op=ALU.add),
                      reads=["pT", "modT"], writes=[nTk])
                continue
            S.add("dve", lambda e, t=t, sl=sl: e.tensor_scalar(out=xn[:tp, :], in0=Xb[:tp, sl, :], scalar1=rs[:tp, t:t + 1], scalar2=None,
                                                         op0=ALU.mult), reads=[("X", sl), "rs"], writes=["xn"])
            for k in range(8):
                S.add("pe", lambda e, k=k: e.transpose(out=pT[:, k, :tp], in_=xn[:tp, k * 128:(k + 1) * 128], identity=ident[:tp, :tp]),
                      reads=["xn", "ident"], writes=["pT"])
            for k in range(8):
                S.add("act", lambda e, k=k, t=t: e.activation(out=nTd[:, k, t * tp:(t + 1) * tp], in_=pT[:, k, :tp], func=AF.Identity,
                                                               scale=GT[:, sub, k, kind:kind + 1], bias=modT[:, 3 * sub, k, kind:kind + 1]),
                      reads=["pT", "GT", "modT"], writes=[nTk])

    def residual_out(blk, t, half, po, pok, slots=None):
        r0, tp, T, kind, own = blk
        t = XS[t] if slots is None else slots[t]
        tm = tmp[half]
        tk = "tmp%d" % half
        S.add("dve", lambda e: e.tensor_tensor(out=tm[:tp, :], in0=po[:tp, :], in1=gateB[:tp, kind, half * 512:(half + 1) * 512], op=ALU.mult),
              reads=[pok, "gateB"], writes=[tk])
        S.add("dve", lambda e: e.tensor_tensor(out=Xb[:tp, t, half * 512:(half + 1) * 512], in0=tm[:tp, :],
                                               in1=Xb[:tp, t, half * 512:(half + 1) * 512], op=ALU.add),
              reads=[tk, ("X", t)], writes=[("X", t)])

    def ffn_phase(idx, sub, blocks, src_p, src_s, dst_fn):
        if idx not in ffn_w:
            ffn_load_weights(idx)
        Wg, Wu, Wd = ffn_w[idx]
        make_gate(sub, 0.5)
        def do_block(blk):
            r0, tp, T, kind, own = blk
            ntok = tp * T
            load_x(src_p, src_s, blk)
            norm_block(blk, sub)
            for f in range(NF):
                pg = pb[f % 2]
                pu = pb[2 + f % 2]
                pgk = "pb%d" % (f % 2)
                puk = "pb%d" % (2 + f % 2)
                for k in range(8):
                    S.add("pe", lambda e, k=k, f=f, pg=pg: e.matmul(out=pg[:, :ntok], lhsT=Wg[:, k, f * 128:(f + 1) * 128], rhs=nT[:, k, :ntok],
                                                                    start=(k == 0), stop=(k == 7)), reads=["Wg", "nT"], writes=[pgk])
                for k in range(8):
                    S.add("pe", lambda e, k=k, f=f, pu=pu: e.matmul(out=pu[:, :ntok], lhsT=Wu[:, k, f * 128:(f + 1) * 128], rhs=nT[:, k, :ntok],
                                                                    start=(k == 0), stop=(k == 7)), reads=["Wu", "nT"], writes=[puk])
                sgt = sg[f % 2]
                sgk = "sg%d" % (f % 2)
                S.add("act", lambda e, pg=pg, sgt=sgt: e.activation(out=sgt[:, :ntok], in_=pg[:, :ntok], func=AF.Silu), reads=[pgk], writes=[sgk])
                S.add("dve", lambda e, f=f, pu=pu, sgt=sgt: e.tensor_tensor(out=HT[:, f, :ntok], in0=pu[:, :ntok], in1=sgt[:, :ntok], op=ALU.mult),
                      reads=[puk, sgk], writes=["HT"])
            for t in range(T):
                for half in range(2):
                    po = pb[4 + half]
                    pok = "pb%d" % (4 + half)
                    for f in range(NF):
                        S.add("pe", lambda e, f=f, t=t, half=half, po=po: e.matmul(out=po[:tp, :], lhsT=HT[:, f, t * tp:(t + 1) * tp],
                                                                               rhs=Wd[:, f, half * 512:(half + 1) * 512],
                                                                               start=(f == 0), stop=(f == NF - 1)), reads=["HT", "Wd"], writes=[pok])
                    residual_out(blk, t, half, po, pok)
                dma(STQ, dst_fn(blk, t), Xb[:tp, XS[t], :], [("X", XS[t])], [])
        for blk in blocks:
            do_block(blk)
        S.barrier()

    def cumsum(LFt, ntile, nvalid_last, base, CPd, ncol_d, tagk):
        n = ntile * 8
        TT = carve(base, [8], F32)
        R = carve(base + 64, [ntile, 8], F32)
        Cs = carve(base + 64 + 4 * n, [ntile, 8], F32)
        r1 = carve(base + 64 + 8 * n, [ntile, 8], F32)
        p3 = [carve(base + 64 + 12 * n + 2 * n * i, [ntile, 8], BF16) for i in range(3)]
        pf = [carve(base + 64 + 18 * n + 4 * n * i, [ntile, 8], F32) for i in range(2)]
        stg = carve(base + 64 + 26 * n, [3, 128], BF16)
        ones = carve(base + 64 + 26 * n + 768, [128], F32)
        triu = carve(base + 64 + 26 * n + 768 + 512, [128], F32)
        sut = carve(base + 64 + 26 * n + 768 + 1024, [128], F32)
        K = tagk
        S.add("dve", lambda e: e.memset(ones, 1.0), writes=[K + "ones"])
        dma("sp", triu, triu_d[:, :], [], [K + "triu"])
        dma("sp", sut, sut_d[:, :], [], [K + "sut"])
        for h in range(8):
            S.add("pe", lambda e, h=h: e.matmul(out=pb[0][:ntile, h:h + 1], lhsT=LFt[:, :, h], rhs=ones[:, 0:1], start=True, stop=True),
                  reads=[K + "ones"], writes=["pb0"])
        S.add("dve", lambda e: e.tensor_copy(out=TT[:ntile, :], in_=pb[0][:ntile, 0:8]), reads=["pb0"], writes=[K + "TT"])
        S.add("dve", lambda e: e.tensor_tensor(out=R[:ntile, :, :], in0=TT[:ntile, :].unsqueeze(1).to_broadcast([ntile, ntile, 8]),
                                               in1=sut[:ntile, :ntile].unsqueeze(2).to_broadcast([ntile, ntile, 8]), op=ALU.mult),
              reads=[K + "TT", K + "sut"], writes=[K + "R"])
        LF2 = LFt.rearrange("p t h -> p (t h)")
        R2 = R.rearrange("p t h -> p (t h)")
        Cs2 = Cs.rearrange("p t h -> p (t h)")
        c0 = 0
        bi = 1
        while c0 < n:
            w = min(512, n - c0)
            pc = pb[bi]
            pck = "pb%d" % bi
            S.add("pe", lambda e, c0=c0, w=w, pc=pc: e.matmul(out=pc[:, :w], lhsT=triu[:, :], rhs=LF2[:, c0:c0 + w], start=True, stop=False),
                  reads=[K + "triu"], writes=[pck])
            S.add("pe", lambda e, c0=c0, w=w, pc=pc: e.matmul(out=pc[:, :w], lhsT=ones[:ntile, :], rhs=R2[:ntile, c0:c0 + w], start=False, stop=True),
                  reads=[K + "R", K + "ones"], writes=[pck])
            S.add("dve", lambda e, c0=c0, w=w, pc=pc: e.tensor_copy(out=Cs2[:, c0:c0 + w], in_=pc[:, :w]), reads=[pck], writes=[K + "Cs"])
            c0 += w
            bi += 1
        S.add("dve", lambda e: e.tensor_copy(out=p3[0], in_=Cs), reads=[K + "Cs"], writes=[K + "p0"])
        S.add("dve", lambda e: e.tensor_copy(out=pf[0], in_=p3[0]), reads=[K + "p0"], writes=[K + "pf0"])
        S.add("dve", lambda e: e.tensor_tensor(out=r1, in0=Cs, in1=pf[0], op=ALU.subtract), reads=[K + "Cs", K + "pf0"], writes=[K + "r1"])
        S.add("dve", lambda e: e.tensor_copy(out=p3[1], in_=r1), reads=[K + "r1"], writes=[K + "p1"])
        S.add("dve", lambda e: e.tensor_copy(out=pf[1], in_=p3[1]), reads=[K + "p1"], writes=[K + "pf1"])
        S.add("dve", lambda e: e.tensor_tensor(out=r1, in0=r1, in1=pf[1], op=ALU.subtract), reads=[K + "r1", K + "pf1"], writes=[K + "r1"])
        S.add("dve", lambda e: e.tensor_copy(out=p3[2], in_=r1), reads=[K + "r1"], writes=[K + "p2"])
        for h in range(8):
            for i in range(3):
                S.add("pe", lambda e, h=h, i=i: e.transpose(out=pT[:ntile, i, :], in_=p3[i][:, :, h], identity=ident[:, :]),
                      reads=[K + "p%d" % i, "ident"], writes=["pT"])
            S.add("act", lambda e: e.copy(out=stg[:ntile, :, :], in_=pT[:ntile, 0:3, :]), reads=["pT"], writes=[K + "stg"])
            for i in range(3):
                dma("sp", CPd[h, i, 0:ntile * 128].rearrange("(t p) -> t p", p=128), stg[:ntile, i, :], [K + "stg"], [K + "CPd"])

    if kstop == 0:
        S.emit(nc, es)
        es.close()
        return nc
    ffn_phase(0, 0, blocks_all, xloc, xsam,
              lambda blk, t: (X1_d[SEQ:SEQ + TS, :] if blk[0] >= SEQ else X1_d[blk[0] + t * 128: blk[0] + (t + 1) * 128, :]))

    if kstop == 1:
        S.emit(nc, es)
        es.close()
        return nc
    Gb = [HT[:, 4 * kd:4 * kd + 4, :].rearrange("p a b -> p (a b)").bitcast(F32) for kd in range(2)]
    for kd in range(2):
        for c in range(8):
            pq = pb[c % 2]
            pqk = "pb%d" % (c % 2)
            S.add("dve", lambda e, c=c, kd=kd: e.tensor_copy(out=gcol[:, :], in_=GT[:, 1, c, kd:kd + 1].to_broadcast([128, 128])),
                  reads=["GT"], writes=["gcol"])
            S.add("pe", lambda e, pq=pq: e.matmul(out=pq[:, 0:128], lhsT=gcol[:, :], rhs=identf[:, :], start=True, stop=True),
                  reads=["gcol", "identf"], writes=[pqk])
            S.add("act", lambda e, pq=pq, c=c, kd=kd: e.copy(out=Gb[kd][:, c * 128:(c + 1) * 128], in_=pq[:, 0:128]), reads=[pqk], writes=["Gb"])
    Win = carve(0, [8, INC], BF16)
    load_weight_rows(Win, w_in, 8, 0, INC, "Win", chunk=1154)
    cbase = 8 * INC * 2
    gkb = carve(cbase, [512], F32)
    gqb = carve(cbase + 2048, [512], F32)
    bfb = carve(cbase + 6144, [8], F32)
    _o = [cbase + 6176]

    def _c(shape, dt):
        n = int(np.prod(shape)) * (2 if dt == BF16 else 4)
        a = carve(_o[0], shape, dt)
        _o[0] += (n + 3) // 4 * 4
        return a

    A2SCR = _o[0]
    bsets = {}
    for nm in ("k0", "k1", "q0", "q1"):
        bsets[nm] = dict(kn=_c([512], F32), sq=_c([512], F32), b16=_c([512], BF16), ssk=_c([8], F32), tag=nm)
    vfs = [_c([512], F32), _c([512], F32)]
    v16s = [_c([512], BF16), _c([512], BF16)]
    zfs = [_c([8], F32), _c([8], F32)]
    KTss = [_c([4, 512], BF16), _c([4, 512], BF16)]
    QTss = [_c([4, 512], BF16), _c([4, 512], BF16)]
    AEND = _o[0]
    ck16 = carve(A2SCR, [NPT, 512], BF16)
    cv16 = carve(A2SCR + NPT * 1024, [NPT, 512], BF16)
    KTs = KTss[0]
    dma("sp", gkb, gk_d[:, :], [], ["gkb"])
    dma("sp", gqb, gq_d[:, :], [], ["gqb"])
    dma("sp", bfb, bf_d[:, :], [], ["bfb"])
    S.add("dve", lambda e: e.tensor_scalar(out=gqb, in0=gqb, scalar1=DH ** -0.5, scalar2=None, op0=ALU.mult), reads=["gqb"], writes=["gqb"])
    S.add("dve", lambda e: e.memset(LFs[:, :, :], 0.0), writes=["sLF"])

    nT2 = carve(AEND, [8, 512], BF16)
    nTbufs = [(nT, "nT"), (nT2, "nT2")]

    pre_slots = {}

    def a2_pre_load(blk, bidx):
        pre_slots[bidx] = load_x(X1_d, X1_d[SEQ:SEQ + TS, :], blk)

    def a2_pre_norm(blk, bidx, tiles):
        tiles = [t for t in tiles if t < blk[2]]
        if tiles:
            norm_block(blk, 1, pre_slots[bidx], nTbufs[bidx % 2][0], nTbufs[bidx % 2][1], lnexp=True, gb=Gb, tiles=tiles)

    def a2_pre(blk, bidx):
        a2_pre_load(blk, bidx)
        a2_pre_norm(blk, bidx, [0, 1, 2, 3])

    pthalf = [0]

    def a2_block(blk, bidx, nxt=None):
        r0, tp, T, kind, own = blk
        sam = r0 >= SEQ
        bp = bidx % 2
        KTs_, QTs_ = KTss[bp], QTss[bp]
        ktk, qtk = "KTs%d" % bp, "QTs%d" % bp
        nT, nTk_ = nTbufs[bp]

        def pair(tl, mid_hook=None):
            ctx = []
            for t in tl:
                tg = (r0 // 128 + t)
                par = tg % 2
                c = dict(t=t, tg=tg, par=par, Bk=bsets["k%d" % par], Bq=bsets["q%d" % par], vf=vfs[par], v16=v16s[par], zf=zfs[par],
                         vfk="vf%d" % par, v16k="v16%d" % par, zfk="zf%d" % par,
                         pk=(pb[0] if par == 0 else pb[4]), pv=(pb[1] if par == 0 else pb[5]), pq=(pb[3] if par == 0 else pb[6]),
                         pkk=("pb0" if par == 0 else "pb4"), pvk=("pb1" if par == 0 else "pb5"), pqk=("pb3" if par == 0 else "pb6"),
                         fc=par * 8, orow=(r0 - (SEQ - LQ)) + t * 128)
                ctx.append(c)

            def proj(c, ps, psk, c0, ncol, o0=0):
                t = c["t"]
                for k in range(8):
                    S.add("pe", lambda e, k=k: e.matmul(out=ps[:tp, o0:o0 + ncol], lhsT=nT[:, k, t * tp:(t + 1) * tp], rhs=Win[:, k, c0:c0 + ncol],
                                                        start=(k == 0), stop=(k == 7)), reads=[nTk_, "Win"], writes=[psk])
            for c in ctx:
                proj(c, c["pk"], c["pkk"], C_K, 512)
                proj(c, c["pv"], c["pvk"], C_V, 512)
                proj(c, pb[2], "pb2", C_F, 8, c["fc"])
                if own:
                    proj(c, c["pq"], c["pqk"], C_Q, 512)
            if mid_hook is not None:
                mid_hook()
            chains = []
            for c in ctx:
                chains.append((c, c["Bk"], c["pk"], c["pkk"], gkb, "gkb"))
                if own:
                    chains.append((c, c["Bq"], c["pq"], c["pqk"], gqb, "gqb"))
            for (c, B, ps, psk, gain, gk_) in chains:
                S.add("act", lambda e, B=B, ps=ps: e.copy(out=B["kn"][:tp, :], in_=ps[:tp, :]), reads=[psk], writes=[B["tag"] + "kn"])
            for c in ctx:
                S.add("act", lambda e, c=c: e.copy(out=c["vf"][:tp, :], in_=c["pv"][:tp, :]), reads=[c["pvk"]], writes=[c["vfk"]])
            for (c, B, ps, psk, gain, gk_) in chains:
                S.add("act", lambda e, B=B: e.activation(out=B["sq"][:tp, :], in_=B["kn"][:tp, :], func=AF.Square),
                      reads=[B["tag"] + "kn"], writes=[B["tag"] + "sq"])
            for c in ctx:
                S.add("dve", lambda e, c=c: e.tensor_tensor(out=c["zf"][:tp, :], in0=pb[2][:tp, c["fc"]:c["fc"] + 8], in1=bfb[:tp, :], op=ALU.add),
                      reads=["pb2", "bfb"], writes=[c["zfk"]])
            for (c, B, ps, psk, gain, gk_) in chains:
                S.add("dve", lambda e, B=B: e.tensor_reduce(out=B["ssk"][:tp, :], in_=B["sq"][:tp, :].rearrange("p (h d) -> p h d", d=DH), axis=AX.X, op=ALU.add),
                      reads=[B["tag"] + "sq"], writes=[B["tag"] + "ssk"])
            for c in ctx:
                S.add("act", lambda e, c=c: e.activation(out=c["zf"][:tp, :], in_=c["zf"][:tp, :], func=AF.Exp, scale=-1.0), reads=[c["zfk"]], writes=[c["zfk"]])
                if own:
                    dma(STQ, (vsn[:, :] if sam else vp[c["orow"]:c["orow"] + 128, :]), c["vf"][:tp, :], [c["vfk"]], [])
                S.add("dve", lambda e, c=c: e.tensor_copy(out=c["v16"][:tp, :], in_=c["vf"][:tp, :]), reads=[c["vfk"]], writes=[c["v16k"]])
                dma(STQ, (Vs_d[PAST:PAST + TS, :] if sam else V_d[r0 + c["t"] * 128:r0 + (c["t"] + 1) * 128, :]), c["v16"][:tp, :], [c["v16k"]], ["V_d"])
            for c in ctx:
                S.add("act", lambda e, c=c: e.activation(out=c["zf"][:tp, :], in_=c["zf"][:tp, :], func=AF.Ln, bias=epsT[:tp, 1:2]),
                      reads=[c["zfk"], "epsT"], writes=[c["zfk"]])
            for (c, B, ps, psk, gain, gk_) in chains:
                S.add("act", lambda e, B=B: e.activation(out=B["ssk"][:tp, :], in_=B["ssk"][:tp, :], func=AF.Ln, scale=1.0 / DH, bias=epsT[:tp, 0:1]),
                      reads=[B["tag"] + "ssk", "epsT"], writes=[B["tag"] + "ssk"])
            for (c, B, ps, psk, gain, gk_) in chains:
                S.add("act", lambda e, B=B: e.activation(out=B["ssk"][:tp, :], in_=B["ssk"][:tp, :], func=AF.Exp, scale=-0.5),
                      reads=[B["tag"] + "ssk"], writes=[B["tag"] + "ssk"])
            for c in ctx:
                lfdst = LFs[:tp, NPT, :] if sam else LF[:, c["tg"], :]
                lfk = "sLF" if sam else ("LF", c["tg"])
                S.add("dve", lambda e, c=c, lfdst=lfdst: e.tensor_scalar(out=lfdst, in0=c["zf"][:tp, :], scalar1=-1.0, scalar2=None, op0=ALU.mult),
                      reads=[c["zfk"]], writes=[lfk])
                if own:
                    dma(STQ, (fsn[:, :] if sam else fp[c["orow"]:c["orow"] + 128, :]), lfdst, [lfk], [])
            for (c, B, ps, psk, gain, gk_) in chains:
                S.add("dve", lambda e, B=B: e.tensor_tensor(out=B["kn"][:tp, :].rearrange("p (h d) -> p h d", d=DH),
                                                          in0=B["kn"][:tp, :].rearrange("p (h d) -> p h d", d=DH),
                                                          in1=B["ssk"][:tp, :].unsqueeze(2).to_broadcast([tp, 8, DH]), op=ALU.mult),
                      reads=[B["tag"] + "kn", B["tag"] + "ssk"], writes=[B["tag"] + "kn"])
            for (c, B, ps, psk, gain, gk_) in chains:
                S.add("dve", lambda e, B=B, gain=gain: e.tensor_tensor(out=B["kn"][:tp, :], in0=B["kn"][:tp, :], in1=gain[:tp, :], op=ALU.mult),
                      reads=[B["tag"] + "kn", gk_], writes=[B["tag"] + "kn"])
            for (c, B, ps, psk, gain, gk_) in chains:
                if own and gk_ == "gkb":
                    dma(STQ, (ksn[:, :] if sam else kp[c["orow"]:c["orow"] + 128, :]), B["kn"][:tp, :], [B["tag"] + "kn"], [])
                S.add("act", lambda e, B=B: e.copy(out=B["b16"][:tp, :], in_=B["kn"][:tp, :]), reads=[B["tag"] + "kn"], writes=[B["tag"] + "b16"])
            for (c, B, ps, psk, gain, gk_) in chains:
                hh = pthalf[0] % 2
                pthalf[0] += 1
                ptk_ = "pT"
                dstT, dstk = (KTs_, ktk) if gk_ == "gkb" else (QTs_, qtk)
                col0 = c["t"] * tp
                for m in range(4):
                    S.add("pe", lambda e, m=m, B=B, hh=hh: e.transpose(out=pT[:, 4 * hh + m, :tp], in_=B["b16"][:tp, m * 128:(m + 1) * 128], identity=ident[:tp, :tp]),
                          reads=[B["tag"] + "b16", "ident"], writes=[ptk_])
                S.add("dve", lambda e, hh=hh, dstT=dstT, col0=col0: e.tensor_copy(out=dstT[:, :, col0:col0 + tp], in_=pT[:, 4 * hh:4 * hh + 4, :tp]),
                      reads=[ptk_], writes=[dstk])
        tiles = list(range(T))
        for i in range(0, T, 2):
            hook = None
            if nxt is not None:
                hook = (lambda i=i: a2_pre_norm(nxt[0], nxt[1], [i, i + 1] if T > 1 else [0, 1, 2, 3]))
            pair(tiles[i:i + 2], hook)
        if sam:
            dma(STQ, KTs_d[:, PAST:PAST + TS].rearrange("(m p) t -> p m t", p=128), KTs_[:, :, :TS], [ktk], ["KT_d"])
            dma(STQ, QTs_d[:, :].rearrange("(m p) t -> p m t", p=128), QTs_[:, :, :TS], [qtk], ["QT_d"])
        else:
            dma(STQ, KT_d[:, r0:r0 + 512].rearrange("(m p) t -> p m t", p=128), KTs_[:, :, :], [ktk], ["KT_d"])
            if own:
                q0 = r0 - (SEQ - LQ)
                dma(STQ, QT_d[:, q0:q0 + 512].rearrange("(m p) t -> p m t", p=128), QTs_[:, :, :], [qtk], ["QT_d"])
    a2_pre(blocks_all[0], 0)
    for bidx, blk in enumerate(blocks_all):
        nxt = None
        if bidx + 1 < len(blocks_all):
            a2_pre_load(blocks_all[bidx + 1], bidx + 1)
            nxt = (blocks_all[bidx + 1], bidx + 1)
        a2_block(blk, bidx, nxt)
        if nxt is not None and blk[2] < nxt[0][2]:
            a2_pre_norm(nxt[0], nxt[1], [2, 3])
    S.barrier()
    for tt in range(NPT):
        dma("pool", ck16[:, tt, :], ck_d[tt * 128:(tt + 1) * 128, :], [], ["ck16"])
        dma("pool", cv16[:, tt, :], cv_d[tt * 128:(tt + 1) * 128, :], [], ["cv16"])
        dma("sp", LFs[:, tt, :], cf_d[tt * 128:(tt + 1) * 128, :], [], ["sLF"])
    dma("sp", Vs_d[0:PAST, :].rearrange("(t p) d -> p t d", p=128), cv16[:, :, :], ["cv16"], ["V_d"])
    for tt in range(NPT):
        for m in range(4):
            S.add("pe", lambda e, m=m, tt=tt: e.transpose(out=pT[:, m, :], in_=ck16[:, tt, m * 128:(m + 1) * 128], identity=ident[:, :]),
                  reads=["ck16", "ident"], writes=["pT"])
        S.add("dve", lambda e: e.tensor_copy(out=KTs[:, :, 0:128], in_=pT[:, 0:4, :]), reads=["pT"], writes=["KTs"])
        dma("sp", KTs_d[:, tt * 128:(tt + 1) * 128].rearrange("(m p) t -> p m t", p=128), KTs[:, :, 0:128], ["KTs"], ["KT_d"])
    if kstop == 2:
        S.emit(nc, es)
        es.close()
        return nc
    S.barrier()
    cumsum(LF, NTL, 128, cbase, CP_d, SEQ, "c")
    S.barrier()
    cumsum(LFs, NPT + 1, 32, cbase, CPs_d, 1152, "s")
    S.barrier()

    if kstop == 3:
        S.emit(nc, es)
        es.close()
        return nc
    KTx = [carve(0, [SEQ], BF16), None]
    Vx = [carve(32768, [NTL, 128], BF16), carve(65536, [NTL, 128], BF16)]
    QTx = carve(98304, [LQ], BF16)
    PTt = [carve(106496, [512], BF16), carve(107520, [512], BF16), carve(116480, [512], BF16), carve(117504, [512], BF16)]
    num = carve(108544, [512], F32)
    rden = carve(110592, [512], F32)
    rden2 = carve(112640, [512], F32)
    ATs = carve(114688, [512], BF16)
    cm16 = carve(115712, [128], BF16)
    cmf = carve(115968, [128], F32)
    dma("sp", cmf, cmask_d[:, :], [], ["cmf"])
    S.add("dve", lambda e: e.tensor_copy(out=cm16, in_=cmf), reads=["cmf"], writes=["cm16"])
    S.add("dve", lambda e: e.memset(Vx[0][:, :, 64:128], 1.0), writes=["Vx0"])
    S.add("dve", lambda e: e.memset(Vx[1][:, :, 0:64], 1.0), writes=["Vx1"])

    def attention(h, Lk, KTsrc, Vsrc, QTsrc, CPsrc, nq_tot, q_c0, qblocks, ATdst):
        par = h % 2
        Vt = Vx[par]
        vk = "Vx%d" % par
        vc0 = 0 if par == 0 else 64
        nkb = (Lk + 127) // 128
        nfull = Lk // 128
        b0 = 0
        while b0 < nfull:
            nb = min(16, nfull - b0)
            dma("pool", Vt[:, b0:b0 + nb, vc0:vc0 + 64],
                Vsrc[b0 * 128:(b0 + nb) * 128, h * 64:(h + 1) * 64].rearrange("(b p) d -> p b d", p=128), ["V_d"], [vk])
            b0 += nb
        if Lk % 128:
            rem = Lk % 128
            dma("pool", Vt[:rem, nfull, vc0:vc0 + 64], Vsrc[nfull * 128:Lk, h * 64:(h + 1) * 64], ["V_d"], [vk])
        dma("sp", KTx[0][0:64, :Lk], KTsrc[h * 64:(h + 1) * 64, 0:Lk], ["KT_d"], ["KTx"])
        S.add("dve", lambda e: e.memset(KTx[0][64:70, :Lk], 1.0), writes=["KTx"])
        dma("sp", KTx[0][67:70, :Lk], CPsrc[h, :, 0:Lk], ["cCPd", "sCPd"], ["KTx"])
        dma("sp", QTx[0:64, :nq_tot], QTsrc[h * 64:(h + 1) * 64, 0:nq_tot], ["QT_d"], ["QTx"])
        S.add("dve", lambda e: e.memset(QTx[64:70, :nq_tot], -1.0), writes=["QTx"])
        dma("sp", QTx[64:67, :nq_tot], CPsrc[h, :, q_c0:q_c0 + nq_tot], ["cCPd", "sCPd"], ["QTx"])
        flat = []
        for qb_i, (q0, nq, steps) in enumerate(qblocks):
            for si, st in enumerate(steps):
                flat.append((qb_i, q0, nq, si, len(steps), st))
        nflat = len(flat)
        nlo = 0 if par == 0 else 64
        dlo = 64 - nlo

        def emit_qk(gi):
            qb_i, q0, nq, si, ns, (k0, nk, qoff, bias, masked) = flat[gi]
            ps = pb[gi % 4]
            psk = "pb%d" % (gi % 4)
            pt = PTt[gi % 4]
            ptk = "PT%d" % (gi % 4)
            w = nq - qoff
            S.add("pe", lambda e: e.matmul(out=ps[:nk, qoff:qoff + w], lhsT=KTx[0][0:70, k0:k0 + nk],
                                           rhs=QTx[0:70, q0 + qoff:q0 + qoff + w], start=True, stop=(not masked)),
                  reads=["KTx", "QTx"], writes=[psk])
            if masked:
                S.add("pe", lambda e: e.matmul(out=ps[:nk, qoff:qoff + nk], lhsT=ident[:nk, :nk], rhs=cm16[:nk, :nk], start=False, stop=True),
                      reads=["ident", "cm16"], writes=[psk])
            if bias is None:
                S.add("act", lambda e: e.activation(out=pt[:nk, qoff:qoff + w], in_=ps[:nk, qoff:qoff + w], func=AF.Exp),
                      reads=[psk], writes=[ptk])
            else:
                S.add("act", lambda e: e.activation(out=pt[:nk, qoff:qoff + w], in_=ps[:nk, qoff:qoff + w], func=AF.Exp,
                                                    bias=validB[:nk, bias:bias + 1]), reads=[psk, "validB"], writes=[ptk])

        def emit_pv(gi):
            qb_i, q0, nq, si, ns, (k0, nk, qoff, bias, masked) = flat[gi]
            pt = PTt[gi % 4]
            ptk = "PT%d" % (gi % 4)
            po = pb[4 + (qb_i % 2)]
            pok = "pb%d" % (4 + (qb_i % 2))
            w = nq - qoff
            kb = k0 // 128
            S.add("pe", lambda e: e.matmul(out=po[:, qoff:qoff + w], lhsT=Vt[:nk, kb, :], rhs=pt[:nk, qoff:qoff + w],
                                           start=(si == 0), stop=(si == ns - 1)), reads=[vk, ptk], writes=[pok])
            if si == ns - 1:
                S.add("dve", lambda e: e.tensor_copy(out=num[nlo:nlo + 64, :nq], in_=po[nlo:nlo + 64, :nq]), reads=[pok], writes=["num"])
                S.add("dve", lambda e: e.reciprocal(out=rden[dlo:dlo + 64, :nq], in_=po[dlo:dlo + 64, :nq]), reads=[pok], writes=["rden"])
                dma("sp", rden2[nlo:nlo + 64, :nq], rden[dlo:dlo + 64, :nq], ["rden"], ["rden2"])
                S.add("dve", lambda e: e.tensor_tensor(out=ATs[nlo:nlo + 64, :nq], in0=num[nlo:nlo + 64, :nq], in1=rden2[nlo:nlo + 64, :nq],
                                                       op=ALU.mult), reads=["num", "rden2"], writes=["ATs"])
                dma("sp", ATdst[h * 64:(h + 1) * 64, q0:q0 + nq], ATs[nlo:nlo + 64, :nq], ["ATs"], ["AT_d"])

        DEPTH = 3
        for gi in range(nflat + DEPTH):
            if gi < nflat:
                emit_qk(gi)
            if gi >= DEPTH:
                emit_pv(gi - DEPTH)

    pq_blocks = []
    for qi in range(NBQ):
        steps = []
        for j in range(3 * NKQ + 4 * qi):
            steps.append((j * 128, 128, 0, (j // NKQ if j < 3 * NKQ else None), False))
        for a in range(4):
            steps.append(((3 * NKQ + 4 * qi + a) * 128, 128, a * 128, None, True))
        pq_blocks.append((qi * 512, 512, steps))
    for h in range(H):
        attention(h, SEQ, KT_d, V_d, QT_d, CP_d, LQ, SEQ - LQ, pq_blocks, AT_d[:, 0:LQ])
    sq_blocks = [(0, TS, [(j * 128, 128, 0, None, False) for j in range(NPT)] + [(PAST, TS, 0, None, True)])]
    ATs_view = AT_d[:, LQ:LQ + TS]
    for h in range(H):
        attention(h, PAST + TS, KTs_d, Vs_d, QTs_d, CPs_d, TS, PAST, sq_blocks, ATs_view)
    S.barrier()

    if kstop == 4:
        S.emit(nc, es)
        es.close()
        return nc
    Win2 = carve(0, [8, 3072], BF16)
    Wpa = carve(49152, [4, D], BF16)
    Wpb = carve(57344, [4, D], BF16)
    Wo = carve(65536, [8, D], BF16)
    c1 = 81920
    WsT = carve(c1, [4, 128], BF16)
    wsf = carve(c1 + 1024, [4, 128], F32)
    bspB = carve(c1 + 3072, [4, 128], F32)
    triu2 = carve(c1 + 5120, [128], F32)
    ggvb2 = carve(c1 + 5632, [512], F32)
    ATb = carve(c1 + 7680, [4, 512], BF16)
    uT = carve(c1 + 11776, [4, 512], BF16)
    BT = carve(c1 + 15872, [4, 512], BF16)
    MT = carve(c1 + 19968, [8, 512], BF16)
    gz = carve(c1 + 28160, [512], F32)
    g3 = carve(c1 + 30208, [512], F32)
    gs = carve(c1 + 32256, [512], F32)
    gvf = carve(c1 + 34304, [512], F32)
    vb16 = carve(c1 + 36352, [512], BF16)
    t1 = carve(c1 + 37376, [512], F32)
    t2 = carve(c1 + 39424, [512], F32)
    mixb = carve(c1 + 41472, [4, 128], F32)
    load_weight_rows(Win2, w_in, 8, C_ZB, 3072, "Win2")
    load_weight_rows(Wpa, wpa_d, 4, 0, D, "Wpa")
    load_weight_rows(Wpb, wpb_d, 4, 0, D, "Wpb")
    load_weight_rows(Wo, wo_d, 8, 0, D, "Wo")
    dma("sp", wsf, wsT_d[:, :, :], [], ["wsf"])
    dma("sp", bspB, bsp_d[:, :, :], [], ["bspB"])
    dma("sp", triu2, triu_d[:, :], [], ["triu2"])
    dma("sp", ggvb2, ggv_d[:, :], [], ["ggvb2"])
    S.add("dve", lambda e: e.tensor_tensor(out=WsT, in0=wsf, in1=triu2.unsqueeze(1).to_broadcast([128, 4, 128]), op=ALU.mult),
          reads=["wsf", "triu2"], writes=["WsT"])
    make_gate(1, 1.0)

    gsets = [dict(gz=gz, g3=g3, gs=gs, kz="gz", k3="g3", ks="gs"),
             dict(gz=wsf.rearrange("p a b -> p (a b)"), g3=tmp[0], gs=tmp[1], kz="wsf", k3="tmp0", ks="tmp1")]
    gcnt = [0]

    def gelu_tanh(dst, ps, psk, rows, n, dstk):
        G_ = gsets[gcnt[0] % 2]
        gcnt[0] += 1
        gz_, g3_, gs_, kz, k3, ks = G_["gz"], G_["g3"], G_["gs"], G_["kz"], G_["k3"], G_["ks"]
        S.add("act", lambda e: e.copy(out=gz_[:rows, :n], in_=ps[:rows, :n]), reads=[psk], writes=[kz])
        S.add("dve", lambda e: e.tensor_tensor(out=g3_[:rows, :n], in0=gz_[:rows, :n], in1=gz_[:rows, :n], op=ALU.mult), reads=[kz], writes=[k3])
        S.add("dve", lambda e: e.tensor_scalar(out=g3_[:rows, :n], in0=g3_[:rows, :n], scalar1=0.044715, scalar2=1.0, op0=ALU.mult, op1=ALU.add),
              reads=[k3], writes=[k3])
        S.add("dve", lambda e: e.tensor_tensor(out=g3_[:rows, :n], in0=g3_[:rows, :n], in1=gz_[:rows, :n], op=ALU.mult), reads=[k3, kz], writes=[k3])
        S.add("act", lambda e: e.activation(out=gs_[:rows, :n], in_=g3_[:rows, :n], func=AF.Sigmoid, scale=1.5957691216057308),
              reads=[k3], writes=[ks])
        S.add("dve", lambda e: e.tensor_tensor(out=dst, in0=gz_[:rows, :n], in1=gs_[:rows, :n], op=ALU.mult), reads=[kz, ks], writes=[dstk])

    nT2c = carve(126976, [8, 512], BF16)
    nTbc = [(nT, "nT"), (nT2c, "nT2")]

    def c1_pre(blk, bidx):
        r0, tp, T, kind, own = blk
        for t in range(T):
            dma("sp", Xb[:tp, 4, :], src_rows(X1_d, X1_d[SEQ:SEQ + TS, :], r0, tp, t), [], [("X", 4)])
            norm_block(blk, 1, [4] * T, nTbc[bidx % 2][0], nTbc[bidx % 2][1], tiles=[t], gb=Gb)
        return list(range(T))

    def c1_block(blk, bidx, slots):
        r0, tp, T, kind, own = blk
        ntok = tp * T
        sam = r0 >= SEQ
        q0 = LQ if sam else r0 - (SEQ - LQ)
        nT, nTk_ = nTbc[bidx % 2]
        for t in range(T):
            dma("sp", Xb[:tp, slots[t], :], src_rows(X1_d, X1_d[SEQ:SEQ + TS, :], r0, tp, t), [], [("X", slots[t])])
        dma("sp", ATb[:, :, :ntok], AT_d[:, q0:q0 + ntok].rearrange("(m p) t -> p m t", p=128), ["AT_d"], ["ATb"])
        for m in range(4):
            ps = pb[m % 2]
            psk = "pb%d" % (m % 2)
            for k in range(8):
                S.add("pe", lambda e, k=k, m=m, ps=ps: e.matmul(out=ps[:, :ntok], lhsT=Win2[:, k, m * 128:(m + 1) * 128], rhs=nT[:, k, :ntok],
                                                                start=(k == 0), stop=(k == 7)), reads=["Win2", nTk_], writes=[psk])
            gelu_tanh(uT[:, m, :ntok], ps, psk, 128, ntok, "uT")
        for t in range(T):
            pvb = pb[2] if t % 2 == 0 else pb[6]
            pvbk = "pb2" if t % 2 == 0 else "pb6"
            for k in range(8):
                S.add("pe", lambda e, k=k, t=t, pvb=pvb: e.matmul(out=pvb[:tp, :], lhsT=nT[:, k, t * tp:(t + 1) * tp], rhs=Win2[:, k, 512:1024],
                                                          start=(k == 0), stop=(k == 7)), reads=["Win2", nTk_], writes=[pvbk])
            gelu_tanh(gvf[:tp, :], pvb, pvbk, tp, 512, "gvf")
            S.add("act", lambda e: e.activation(out=t1[:tp, :], in_=gvf[:tp, :], func=AF.Square, accum_out=ss[:tp, 4:5]), reads=["gvf"], writes=["t1", "ss"])
            S.add("act", lambda e: e.activation(out=rs[:tp, 4:5], in_=ss[:tp, 4:5], func=AF.Sqrt, scale=1.0 / 512, bias=epsT[:tp, 0:1]),
                  reads=["ss", "epsT"], writes=["rs"])
            S.add("dve", lambda e: e.reciprocal(out=rs[:tp, 4:5], in_=rs[:tp, 4:5]), reads=["rs"], writes=["rs"])
            S.add("dve", lambda e: e.scalar_tensor_tensor(out=gvf[:tp, :], in0=gvf[:tp, :], scalar=rs[:tp, 4:5], in1=ggvb2[:tp, :],
                                                          op0=ALU.mult, op1=ALU.mult), reads=["gvf", "rs", "ggvb2"], writes=["gvf"])
            if sam:
                dma(STQ, gvs[:, :], gvf[:tp, :], ["gvf"], [])
            S.add("act", lambda e: e.copy(out=vb16[:tp, :], in_=gvf[:tp, :]), reads=["gvf"], writes=["vb16"])
            for g in range(4):
                S.add("pe", lambda e, g=g: e.matmul(out=pb[3][:, g * 128:g * 128 + tp], lhsT=vb16[:tp, g * 128:(g + 1) * 128], rhs=WsT[:tp, g, :tp],
                                                    start=True, stop=True), reads=["vb16", "WsT"], writes=["pb3"])
            S.add("dve", lambda e: e.tensor_tensor(out=mixb[:, :, :tp], in0=pb[3][:, :].rearrange("p (g t) -> p g t", t=128)[:, :, :tp],
                                                   in1=bspB[:, :, :tp], op=ALU.add), reads=["pb3", "bspB"], writes=["mixb"])
            S.add("dve", lambda e, t=t: e.tensor_tensor(out=BT[:, :, t * tp:(t + 1) * tp], in0=mixb[:, :, :tp], in1=uT[:, :, t * tp:(t + 1) * tp], op=ALU.mult),
                  reads=["mixb", "uT"], writes=["BT"])
        for m in range(8):
            for br in range(2):
                pgt = pb[br * 2]
                pmt = pb[br * 2 + 1]
                pgk = "pb%d" % (br * 2)
                pmk = "pb%d" % (br * 2 + 1)
                gc0 = (1024 if br == 0 else 2048) + m * 128
                Wp = Wpa if br == 0 else Wpb
                wpk = "Wpa" if br == 0 else "Wpb"
                act_in = ATb if br == 0 else BT
                aik = "ATb" if br == 0 else "BT"
                for k in range(8):
                    S.add("pe", lambda e, k=k, pgt=pgt, gc0=gc0: e.matmul(out=pgt[:, :ntok], lhsT=Win2[:, k, gc0:gc0 + 128], rhs=nT[:, k, :ntok],
                                                                        start=(k == 0), stop=(k == 7)), reads=["Win2", nTk_], writes=[pgk])
                for k in range(4):
                    S.add("pe", lambda e, k=k, pmt=pmt, Wp=Wp, act_in=act_in, m=m: e.matmul(out=pmt[:, :ntok], lhsT=Wp[:, k, m * 128:(m + 1) * 128],
                                                                                          rhs=act_in[:, k, :ntok], start=(k == 0), stop=(k == 3)),
                          reads=[wpk, aik], writes=[pmk])
                sgt = sg[br]
                sgk = "sg%d" % br
                S.add("act", lambda e, pgt=pgt, sgt=sgt: e.activation(out=sgt[:, :ntok], in_=pgt[:, :ntok], func=AF.Sigmoid), reads=[pgk], writes=[sgk])
                tt_ = t1 if br == 0 else t2
                ttk = "t1" if br == 0 else "t2"
                S.add("dve", lambda e, pmt=pmt, sgt=sgt, tt_=tt_: e.tensor_tensor(out=tt_[:, :ntok], in0=pmt[:, :ntok], in1=sgt[:, :ntok], op=ALU.mult),
                      reads=[pmk, sgk], writes=[ttk])
            S.add("dve", lambda e, m=m: e.tensor_tensor(out=MT[:, m, :ntok], in0=t1[:, :ntok], in1=t2[:, :ntok], op=ALU.add),
                  reads=["t1", "t2"], writes=["MT"])
        for t in range(T):
            for half in range(2):
                po = pb[4 + half]
                pok = "pb%d" % (4 + half)
                for k in range(8):
                    S.add("pe", lambda e, k=k, t=t, half=half, po=po: e.matmul(out=po[:tp, :], lhsT=MT[:, k, t * tp:(t + 1) * tp],
                                                                           rhs=Wo[:, k, half * 512:(half + 1) * 512], start=(k == 0), stop=(k == 7)),
                          reads=["MT", "Wo"], writes=[pok])
                residual_out(blk, t, half, po, pok, slots)
            dma(STQ, (X2_d[LQ:LQ + TS, :] if sam else X2_d[q0 + t * 128:q0 + (t + 1) * 128, :]), Xb[:tp, slots[t], :], [("X", slots[t])], [])
    _sl = c1_pre(blocks_own[0], 0)
    for bidx, blk in enumerate(blocks_own):
        _cur = _sl
        if bidx + 1 < len(blocks_own):
            _sl = c1_pre(blocks_own[bidx + 1], bidx + 1)
        c1_block(blk, bidx, _cur)
    S.barrier()

    if kstop == 5:
        S.emit(nc, es)
        es.close()
        return nc
    own_blocks2 = [((r0 - (SEQ - LQ)) if r0 < SEQ else SEQ, tp, T, kind, own) for (r0, tp, T, kind, own) in blocks_own]
    ffn_phase(1, 2, own_blocks2, X2_d, X2_d[LQ:LQ + TS, :],
              lambda blk, t: (ys[:, :] if blk[0] >= SEQ else yp[blk[0] + t * 128: blk[0] + (t + 1) * 128, :]))

    S.emit(nc, es)
    es.close()
    return nc


_NC_CACHE = {}
_KSTOP = 99


def _bf(x):
    return np.ascontiguousarray(x, dtype=np.float32)


def kernel(x_prompt, x_sample, c_prompt, c_sample, cache_fox_k, cache_fox_v, cache_fox_logf,
           w_ada, b_ada, g_norm_ffn1, w_ffn1_gate, w_ffn1_up, w_ffn1_down,
           g_norm_mix, w_in, b_forget, g_q, g_k, g_gmlp_v, w_spatial, b_spatial,
           w_proj_a, w_proj_b, w_out, g_norm_ffn2, w_ffn2_gate, w_ffn2_up, w_ffn2_down):
    f = lambda a: np.asarray(a, dtype=np.float32)
    x_prompt, x_sample, c_prompt, c_sample = f(x_prompt), f(x_sample), f(c_prompt), f(c_sample)
    cache_fox_k, cache_fox_v, cache_fox_logf = f(cache_fox_k), f(cache_fox_v), f(cache_fox_logf)
    if "nc" not in _NC_CACHE:
        _NC_CACHE["nc"] = build_program(_KSTOP)
    nc = _NC_CACHE["nc"]
    ident = np.eye(128, dtype=np.float32)
    idx = np.arange(128)
    cmask = np.where(idx[:, None] > idx[None, :], NEG, 0.0).astype(np.float32)
    triu = (idx[:, None] <= idx[None, :]).astype(np.float32)
    sut = (idx[:, None] < idx[None, :]).astype(np.float32)
    rep = lambda v, n: np.ascontiguousarray(np.broadcast_to(np.tile(f(v), n)[None, :], (128, v.size * n)))
    gT = np.stack([f(g_norm_ffn1)[0].reshape(8, 128).T, f(g_norm_mix)[0].reshape(8, 128).T, f(g_norm_ffn2)[0].reshape(8, 128).T], axis=1)
    common = {
        "w_ada": _bf(f(w_ada)[0]), "b_ada": _bf(f(b_ada)), "gT": _bf(gT),
        "wg1": _bf(f(w_ffn1_gate)[0]), "wu1": _bf(f(w_ffn1_up)[0]), "wd1": _bf(f(w_ffn1_down)[0]),
        "wg2": _bf(f(w_ffn2_gate)[0]), "wu2": _bf(f(w_ffn2_up)[0]), "wd2": _bf(f(w_ffn2_down)[0]),
        "w_in": _bf(f(w_in)[0]),
        "bfb": rep(f(b_forget)[0], 1), "gqb": rep(f(g_q)[0], 8), "gkb": rep(f(g_k)[0], 8), "ggvb": rep(f(g_gmlp_v)[0], 1),
        "wsT": _bf(np.transpose(f(w_spatial)[0], (2, 0, 1))),
        "bspb": _bf(np.broadcast_to(f(b_spatial)[0][None, :, :], (128, 4, 128))),
        "wpa": _bf(f(w_proj_a)[0]), "wpb": _bf(f(w_proj_b)[0]), "wo": _bf(f(w_out)[0]),
        "ident": ident, "cmask": cmask, "triu": triu, "sut": sut,
    }
    in_maps = []
    for c in range(8):
        b, g = c // 4, c % 4
        order = [(g + 1) % 4, (g + 2) % 4, (g + 3) % 4, g]
        xloc = np.concatenate([x_prompt[b, q * LQ:(q + 1) * LQ] for q in order], axis=0)
        valid = np.zeros((128, 3), np.float32)
        for s in range(3):
            if order[s] > g:
                valid[:, s] = -1e30
        cvec = np.stack([c_prompt[b], c_sample[c]], axis=0)
        cT = np.ascontiguousarray(cvec.reshape(2, 8, 128).transpose(2, 1, 0))
        m = dict(common)
        m.update({
            "xloc": _bf(xloc), "xsam": _bf(x_sample[c]), "cT": _bf(cT), "valid": valid,
            "ck": _bf(cache_fox_k[0, c].reshape(PAST, 512)), "cv": _bf(cache_fox_v[0, c].reshape(PAST, 512)),
            "cf": _bf(cache_fox_logf[0, c]),
        })
        in_maps.append(m)
    res = run_bass_kernel_spmd(nc, in_maps, core_ids=list(range(8)))
    R = res.results
    B = 2
    y_prompt = np.zeros((B, SEQ, D), np.float32)
    new_k = np.zeros((1, B, SEQ, H, DH), np.float32)
    new_v = np.zeros((1, B, SEQ, H, DH), np.float32)
    new_f = np.zeros((1, B, SEQ, H), np.float32)
    y_sample = np.zeros((8, TS, D), np.float32)
    ks = np.zeros((1, 8, TS, H, DH), np.float32)
    vs = np.zeros((1, 8, TS, H, DH), np.float32)
    fs = np.zeros((1, 8, TS, H), np.float32)
    gv = np.zeros((1, 8, TS, 512), np.float32)
    for c in range(8):
        b, g = c // 4, c % 4
        sl = slice(g * LQ, (g + 1) * LQ)
        y_prompt[b, sl] = R[c]["yp"]
        new_k[0, b, sl] = R[c]["kp"].reshape(LQ, H, DH)
        new_v[0, b, sl] = R[c]["vp"].reshape(LQ, H, DH)
        new_f[0, b, sl] = R[c]["fp"]
        y_sample[c] = R[c]["ys"]
        ks[0, c] = R[c]["ksn"].reshape(TS, H, DH)
        vs[0, c] = R[c]["vsn"].reshape(TS, H, DH)
        fs[0, c] = R[c]["fsn"]
        gv[0, c] = R[c]["gvs"]
    return (y_prompt, y_sample, new_k, new_v, new_f, ks, vs, fs, gv)
```
